# Optimizing a Trainium2 kernel written in Bass

```python
import jax, jax.numpy as jnp
from jax import lax
import numpy as np

D_MODEL = 1024
BATCH = 4
SEQ = 4096
DEPTH = 4

HG_HEADS = 8
HG_HEAD_DIM = 128
HG_WIDTH = HG_HEADS * HG_HEAD_DIM
HG_CHUNK = 64
LRU_WIDTH = D_MODEL
LRU_HEADS = 8
LRU_HEAD_DIM = LRU_WIDTH // LRU_HEADS
LRU_CONV = 4
LRU_C = 8.0
CONV_WIDTH = D_MODEL
CONV_KERNEL = 31
N_BRANCH = 3
D_FF = 2816
MIX_COLS = (4 * HG_WIDTH) + (2 * LRU_WIDTH) + (2 * CONV_WIDTH) + (N_BRANCH * D_MODEL)

kernel_name = "hybrid_hgrn2_rglru_conformer_macaron"


def rmsnorm(x, g, eps=1e-6):
    xf = x.astype(jnp.float32)
    y = xf * lax.rsqrt(jnp.mean(xf * xf, axis=-1, keepdims=True) + eps)
    return (y * g.astype(jnp.float32)).astype(x.dtype)


def layernorm(x, g, b, eps=1e-5):
    xf = x.astype(jnp.float32)
    mu = jnp.mean(xf, axis=-1, keepdims=True)
    xc = xf - mu
    y = xc * lax.rsqrt(jnp.mean(xc * xc, axis=-1, keepdims=True) + eps)
    return (y * g.astype(jnp.float32) + b.astype(jnp.float32)).astype(x.dtype)


def swiglu_ffn(x, w_in, w_out):
    g, u = jnp.split(x @ w_in, 2, axis=-1)
    return (jax.nn.silu(g) * u) @ w_out


def causal_dwconv(x, w, b):
    k = w.shape[0]
    xp = jnp.pad(x, ((0, 0), (k - 1, 0), (0, 0)))
    y = lax.conv_general_dilated(
        xp, w[:, None, :].astype(x.dtype), window_strides=(1,), padding='VALID',
        dimension_numbers=('NWC', 'WIO', 'NWC'), feature_group_count=x.shape[-1])
    return y + b


def hgrn2_chunked(q, k, v, log_f):
    b_, s_, h_, dk = q.shape
    dv = v.shape[-1]
    n = s_ // HG_CHUNK

    def to_chunks(t):
        return t.reshape(b_, n, HG_CHUNK, h_, t.shape[-1]).transpose(1, 0, 3, 2, 4)

    qc, kc, vc, gc = to_chunks(q), to_chunks(k), to_chunks(v), to_chunks(log_f)
    causal = jnp.tril(jnp.ones((HG_CHUNK, HG_CHUNK), dtype=bool))[:, :, None]

    def step(state, inp):
        qb, kb, vb, gb = inp
        bcum = jnp.cumsum(gb, axis=2)
        diff = bcum[:, :, :, None, :] - bcum[:, :, None, :, :]
        decay = jnp.exp(jnp.where(causal, diff, -jnp.inf))
        scores = jnp.einsum('bhtk,bhsk,bhtsk->bhts', qb, kb, decay)
        o = (jnp.einsum('bhts,bhsv->bhtv', scores, vb)
             + jnp.einsum('bhtk,bhkv->bhtv', qb * jnp.exp(bcum), state))
        b_last = bcum[:, :, -1:, :]
        state = (jnp.exp(b_last[:, :, 0, :])[..., None] * state
                 + jnp.einsum('bhsk,bhsv->bhkv', kb * jnp.exp(b_last - bcum), vb))
        return state, o

    s0 = jnp.zeros((b_, h_, dk, dv), jnp.float32)
    _, o = lax.scan(step, s0, (qc, kc, vc, gc))
    return o.transpose(1, 0, 3, 2, 4).reshape(b_, s_, h_, dv)


def linear_scan(a, u):
    def combine(c1, c2):
        a1, b1 = c1
        a2, b2 = c2
        return a1 * a2, a2 * b1 + b2
    _, h = lax.associative_scan(combine, (a, u), axis=1)
    return h


def hybrid_mixer(u, lb, w_in, b_in, hg_norm, lru_conv_w, lru_conv_b, lru_gate_w, lru_gate_b,
                 lru_lambda, cv_dw_w, cv_dw_b, cv_ln_g, cv_ln_b, w_branch, w_out):
    bsz, s_, _ = u.shape
    f32 = jnp.float32
    proj = u @ w_in + b_in
    sizes = [HG_WIDTH] * 4 + [LRU_WIDTH] * 2 + [CONV_WIDTH] * 2 + [D_MODEL] * N_BRANCH
    cuts = [int(c) for c in np.cumsum(sizes)[:-1]]
    hq, hf, hi, hg, lx, lg, ca, cb, g_a, g_b, g_c = jnp.split(proj, cuts, axis=-1)

    hshape = (bsz, s_, HG_HEADS, HG_HEAD_DIM)
    q = jax.nn.silu(hq.astype(f32)).reshape(hshape)
    z = hf.astype(f32).reshape(hshape)
    lb_h = lb.astype(f32).reshape(HG_HEADS, HG_HEAD_DIM)
    f = lb_h + (1.0 - lb_h) * jax.nn.sigmoid(z)
    k = (1.0 - lb_h) * jax.nn.sigmoid(-z)
    v = hi.astype(f32).reshape(hshape)
    o_hg = hgrn2_chunked(q, k, v, jnp.log(f))
    y_hg = rmsnorm(o_hg, hg_norm).reshape(bsz, s_, HG_WIDTH).astype(u.dtype) * jax.nn.silu(hg)

    xb = causal_dwconv(lx, lru_conv_w, lru_conv_b)
    xh = xb.reshape(bsz, s_, LRU_HEADS, LRU_HEAD_DIM)
    gates = jnp.einsum('bshi,ghij->gbshj', xh, lru_gate_w).reshape(2, bsz, s_, LRU_WIDTH)
    gates = gates.astype(f32) + lru_gate_b.astype(f32)[:, None, None, :]
    r_t = jax.nn.sigmoid(gates[0])
    i_t = jax.nn.sigmoid(gates[1])
    log_a = -LRU_C * r_t * jax.nn.softplus(-lru_lambda.astype(f32))
    a_t = jnp.exp(log_a)
    mult = jnp.sqrt(-jnp.expm1(2.0 * log_a))
    h = linear_scan(a_t, mult * (i_t * xb.astype(f32)))
    y_lru = h.astype(u.dtype) * jax.nn.gelu(lg)

    cu = ca * jax.nn.sigmoid(cb)
    cu = causal_dwconv(cu, cv_dw_w, cv_dw_b)
    y_cv = jax.nn.silu(layernorm(cu, cv_ln_g, cv_ln_b))

    merged = (jax.nn.sigmoid(g_a) * (y_hg @ w_branch[0])
              + jax.nn.sigmoid(g_b) * (y_lru @ w_branch[1])
              + jax.nn.sigmoid(g_c) * (y_cv @ w_branch[2]))
    return merged @ w_out


def setup_inputs(seed: int = 0) -> dict:
    key = jax.random.key(seed)
    ks = jax.random.split(key, 24)
    L, D = DEPTH, D_MODEL

    def nrm(k, shape, fan_in):
        return jax.random.normal(k, shape, jnp.float32) * (fan_in ** -0.5)

    def gain(k, shape):
        return 1.0 + 0.01 * jax.random.normal(k, shape, jnp.float32)

    a_c = jax.random.uniform(ks[14], (L, LRU_WIDTH), jnp.float32, minval=0.9, maxval=0.999)
    sig = a_c ** (1.0 / LRU_C)
    lru_lambda = jnp.log(sig) - jnp.log1p(-sig)

    return {
        "x": jax.random.normal(ks[0], (BATCH, SEQ, D), jnp.float32),
        "norm_ffn1": gain(ks[1], (L, D)),
        "ffn1_w_in": nrm(ks[2], (L, D, 2 * D_FF), D),
        "ffn1_w_out": nrm(ks[3], (L, D_FF, D), D_FF),
        "norm_mix": gain(ks[4], (L, D)),
        "w_in_mix": nrm(ks[5], (L, D, MIX_COLS), D),
        "b_in_mix": 0.01 * jax.random.normal(ks[6], (L, MIX_COLS), jnp.float32),
        "hgrn_lb_logits": 0.5 * jax.random.normal(ks[7], (L, HG_WIDTH), jnp.float32),
        "hg_norm": gain(ks[8], (L, HG_HEAD_DIM)),
        "lru_conv_w": nrm(ks[9], (L, LRU_CONV, LRU_WIDTH), LRU_CONV),
        "lru_conv_b": 0.01 * jax.random.normal(ks[10], (L, LRU_WIDTH), jnp.float32),
        "lru_gate_w": nrm(ks[11], (L, 2, LRU_HEADS, LRU_HEAD_DIM, LRU_HEAD_DIM), LRU_HEAD_DIM),
        "lru_gate_b": 0.01 * jax.random.normal(ks[12], (L, 2, LRU_WIDTH), jnp.float32),
        "lru_lambda": lru_lambda,
        "cv_dw_w": nrm(ks[15], (L, CONV_KERNEL, CONV_WIDTH), CONV_KERNEL),
        "cv_dw_b": 0.01 * jax.random.normal(ks[16], (L, CONV_WIDTH), jnp.float32),
        "cv_ln_g": gain(ks[17], (L, CONV_WIDTH)),
        "cv_ln_b": 0.01 * jax.random.normal(ks[18], (L, CONV_WIDTH), jnp.float32),
        "w_branch": nrm(ks[19], (L, N_BRANCH, D, D), D),
        "w_out_mix": nrm(ks[20], (L, D, D), D),
        "norm_ffn2": gain(ks[21], (L, D)),
        "ffn2_w_in": nrm(ks[22], (L, D, 2 * D_FF), D),
        "ffn2_w_out": nrm(ks[23], (L, D_FF, D), D_FF),
        "norm_final": gain(ks[13], (D,)),
    }


def reference(x, norm_ffn1, ffn1_w_in, ffn1_w_out, norm_mix, w_in_mix, b_in_mix, hgrn_lb_logits,
              hg_norm, lru_conv_w, lru_conv_b, lru_gate_w, lru_gate_b, lru_lambda, cv_dw_w, cv_dw_b,
              cv_ln_g, cv_ln_b, w_branch, w_out_mix, norm_ffn2, ffn2_w_in, ffn2_w_out, norm_final):
    lb_all = jax.nn.softmax(hgrn_lb_logits.astype(jnp.float32), axis=0)
    lb_all = jnp.cumsum(lb_all, axis=0) - lb_all[:1]
    for l in range(DEPTH):
        x = x + 0.5 * swiglu_ffn(rmsnorm(x, norm_ffn1[l]), ffn1_w_in[l], ffn1_w_out[l])
        x = x + hybrid_mixer(rmsnorm(x, norm_mix[l]), lb_all[l], w_in_mix[l], b_in_mix[l],
                             hg_norm[l], lru_conv_w[l], lru_conv_b[l], lru_gate_w[l], lru_gate_b[l],
                             lru_lambda[l], cv_dw_w[l], cv_dw_b[l], cv_ln_g[l], cv_ln_b[l],
                             w_branch[l], w_out_mix[l])
        x = x + 0.5 * swiglu_ffn(rmsnorm(x, norm_ffn2[l]), ffn2_w_in[l], ffn2_w_out[l])
    return rmsnorm(x, norm_final)
```

```python
import numpy as np
from collections import deque

import concourse.bass as bass
import concourse.mybir as mybir
from concourse.bass_utils import run_bass_kernel_spmd

F32 = mybir.dt.float32
BF16 = mybir.dt.bfloat16
AF = mybir.ActivationFunctionType
ALU = mybir.AluOpType

D = 1024
DFF = 2816
NFF = DFF // 128
MIXC = 11264
NBLK = MIXC // 128
DEPTH = 4
RAW_WINDOW = 6


class Tile:
    __slots__ = ("ap", "last_w", "readers", "name")

    def __init__(self, ap, name=""):
        self.ap = ap
        self.last_w = None
        self.readers = {}
        self.name = name

    def __getitem__(self, k):
        return TV(self, self.ap[k])

    @property
    def v(self):
        return TV(self, self.ap)


class TV:
    __slots__ = ("tile", "ap")

    def __init__(self, tile, ap):
        self.tile = tile
        self.ap = ap

    def __getitem__(self, k):
        return TV(self.tile, self.ap[k])

    def bc(self, shape):
        return TV(self.tile, self.ap.to_broadcast(list(shape)))


class DSem:
    def __init__(self, sem):
        self.sem = sem
        self.count = 0


class Op:
    __slots__ = ("eng", "fn", "deps", "dma", "dsem", "done", "signal", "sig", "pos", "dwait", "sid")

    def __init__(self, eng, fn, dma=False, dsem=None):
        self.eng = eng
        self.fn = fn
        self.dma = dma
        self.dsem = dsem
        self.deps = []
        self.dwait = []
        self.done = 0
        self.signal = False
        self.sig = 0
        self.pos = 0
        self.sid = 0


class FreeList:
    def __init__(self, items):
        self.free = deque(items)

    def get(self):
        if not self.free:
            raise RuntimeError("pool exhausted")
        return self.free.popleft()

    def put(self, t):
        self.free.append(t)


def _ap(a):
    return a.ap if isinstance(a, TV) else a


def _tiles(*args):
    out = []
    for a in args:
        if isinstance(a, TV):
            out.append(a.tile)
    return out


class Prog:
    ENGS = ("pe", "act", "dve", "pool", "sp")

    def __init__(self, nc):
        self.nc = nc
        self.ops = {e: [] for e in self.ENGS}
        self.nops = 0
        self.cur = None
        self.sid = 0
        self.nsid = 0
        self.spos = {}

    def rec(self, eng, fn, reads, writes, dma=False, dsem=None):
        op = Op(eng, fn, dma, dsem)
        op.sid = self.sid
        if self.cur is None:
            op.pos = len(self.ops[eng])
        else:
            op.pos = self.spos.get(eng, 0)
            self.spos[eng] = op.pos + 1
        deps = {}
        raw = set()
        for t in reads:
            if t.last_w is not None:
                deps[id(t.last_w)] = t.last_w
                raw.add(id(t.last_w))
        for t in writes:
            if t.last_w is not None:
                deps[id(t.last_w)] = t.last_w
            for r in t.readers.values():
                deps[id(r)] = r
        for d in deps.values():
            if d is op:
                continue
            if d.dma:
                op.dwait.append((d.dsem, d.dsem.count))
                continue
            if (not dma) and d.eng == eng:
                if eng == "pe":
                    continue
                if id(d) not in raw or (d.sid == op.sid and op.pos - d.pos > RAW_WINDOW):
                    continue
            d.signal = True
            op.deps.append(d)
        if dma:
            dsem.count += 16
            op.done = dsem.count
        for t in writes:
            t.last_w = op
            t.readers = {}
        wset = set(id(t) for t in writes)
        for t in reads:
            if id(t) not in wset:
                key = id(dsem) if dma else eng
                t.readers[key] = op
        if self.cur is None:
            self.ops[eng].append(op)
        else:
            self.cur.append(op)
        self.nops += 1
        return op

    def begin_stream(self):
        self.nsid += 1
        self.sid = self.nsid
        self.cur = []
        self.spos = {}

    def end_stream(self):
        st = self.cur
        self.cur = None
        self.sid = 0
        return st

    def merge(self, streams):
        streams = [st for st in streams if st]
        idx = [0] * len(streams)
        while True:
            best = -1
            bf = 2.0
            for i, st in enumerate(streams):
                if idx[i] < len(st):
                    f = idx[i] / len(st)
                    if f < bf:
                        bf = f
                        best = i
            if best < 0:
                break
            op = streams[best][idx[best]]
            idx[best] += 1
            self.ops[op.eng].append(op)

    def mm(self, out, lhsT, rhs, start, stop):
        o, l, r = _ap(out), _ap(lhsT), _ap(rhs)
        return self.rec("pe", lambda e: e.matmul(o, l, r, start=start, stop=stop),
                        _tiles(lhsT, rhs), _tiles(out))

    def transpose(self, out, in_, ident):
        o, i, d = _ap(out), _ap(in_), _ap(ident)
        return self.rec("pe", lambda e: e.transpose(o, i, d), _tiles(in_, ident), _tiles(out))

    def act(self, out, in_, func, bias=0.0, scale=1.0):
        o, i, b, s = _ap(out), _ap(in_), _ap(bias), _ap(scale)
        return self.rec("act", lambda e: e.activation(out=o, in_=i, func=func, bias=b, scale=s),
                        _tiles(in_, bias, scale), _tiles(out))

    def tt(self, eng, out, in0, in1, op):
        o, a, b = _ap(out), _ap(in0), _ap(in1)
        return self.rec(eng, lambda e: e.tensor_tensor(out=o, in0=a, in1=b, op=op),
                        _tiles(in0, in1), _tiles(out))

    def ts(self, eng, out, in0, s1, s2, op0, op1=None):
        o, a, x1, x2 = _ap(out), _ap(in0), _ap(s1), _ap(s2)
        if op1 is None:
            fn = lambda e: e.tensor_scalar(out=o, in0=a, scalar1=x1, scalar2=None, op0=op0)
        else:
            fn = lambda e: e.tensor_scalar(out=o, in0=a, scalar1=x1, scalar2=x2, op0=op0, op1=op1)
        return self.rec(eng, fn, _tiles(in0, s1, s2), _tiles(out))

    def stt(self, eng, out, in0, scalar, in1, op0, op1):
        o, a, s, b = _ap(out), _ap(in0), _ap(scalar), _ap(in1)
        return self.rec(eng, lambda e: e.scalar_tensor_tensor(out=o, in0=a, scalar=s, in1=b, op0=op0, op1=op1),
                        _tiles(in0, scalar, in1), _tiles(out))

    def scan(self, eng, out, d0, d1, init, op0, op1):
        o, a, b, i = _ap(out), _ap(d0), _ap(d1), _ap(init)
        return self.rec(eng, lambda e: e.tensor_tensor_scan(out=o, data0=a, data1=b, initial=i, op0=op0, op1=op1),
                        _tiles(d0, d1, init), _tiles(out))

    def copy(self, eng, out, in_):
        o, i = _ap(out), _ap(in_)
        if eng == "act":
            return self.rec("act", lambda e: e.activation(out=o, in_=i, func=AF.Copy), _tiles(in_), _tiles(out))
        return self.rec(eng, lambda e: e.tensor_copy(out=o, in_=i), _tiles(in_), _tiles(out))

    def memset(self, eng, out, val):
        o = _ap(out)
        return self.rec(eng, lambda e: e.memset(o, val), [], _tiles(out))

    def reduce_sum(self, eng, out, in_):
        o, i = _ap(out), _ap(in_)
        return self.rec(eng, lambda e: e.reduce_sum(out=o, in_=i, axis=mybir.AxisListType.X), _tiles(in_), _tiles(out))

    def recip(self, out, in_):
        o, i = _ap(out), _ap(in_)
        return self.rec("dve", lambda e: e.reciprocal(out=o, in_=i), _tiles(in_), _tiles(out))

    def dma(self, q, out, in_, dsem):
        o, i = _ap(out), _ap(in_)
        return self.rec(q, lambda e: e.dma_start(out=o, in_=i), _tiles(in_), _tiles(out), dma=True, dsem=dsem)

    def emit(self, block, sems, final_waits):
        nc = self.nc
        for e in self.ENGS:
            c = 0
            for op in self.ops[e]:
                if op.signal:
                    c += 1
                    op.sig = c

        def run(engname, eng):
            waited = {}
            for op in self.ops[engname]:
                for d in op.deps:
                    key = d.eng
                    if waited.get(key, 0) < d.sig:
                        eng.wait_ge(sems[d.eng], d.sig)
                        waited[key] = d.sig
                for (ds, val) in op.dwait:
                    key = id(ds)
                    if waited.get(key, 0) < val:
                        eng.wait_ge(ds.sem, val)
                        waited[key] = val
                ins = op.fn(eng)
                if op.dma:
                    ins.then_inc(op.dsem.sem, 16)
                elif op.signal:
                    ins.then_inc(sems[engname], 1)
            if engname == "sp":
                for ds in final_waits:
                    eng.wait_ge(ds.sem, ds.count)

        @block.tensor
        def _(eng):
            run("pe", eng)

        @block.scalar
        def _(eng):
            run("act", eng)

        @block.vector
        def _(eng):
            run("dve", eng)

        @block.gpsimd
        def _(eng):
            run("pool", eng)

        @block.sync
        def _(eng):
            run("sp", eng)


def _param_layout():
    off = {}
    c = 0
    for l in range(DEPTH):
        for name, n in (("g1", 8), ("gm", 8), ("g2", 8), ("bmix", NBLK), ("hgn", 1), ("lcw", 32),
                        ("lcb", 8), ("lgb", 16), ("lam", 8), ("cvw", 8 * 31), ("cvb", 8), ("lng", 8),
                        ("lnb", 8)):
            off[(name, l)] = c
            c += n
    off["gf"] = c
    c += 8
    off["logits"] = c
    c += 32
    off["lam_all"] = c
    c += 32
    return off, c


POFF, NPAR = _param_layout()


def _fm(v):
    v = np.asarray(v, dtype=np.float32)
    return np.ascontiguousarray(v.reshape(-1, 128).T)


def pack_params(inp):
    P = np.zeros((128, NPAR), np.float32)

    def put(key, arr):
        o = POFF[key]
        P[:, o:o + arr.shape[1]] = arr

    for l in range(DEPTH):
        put(("g1", l), _fm(inp["norm_ffn1"][l]))
        put(("gm", l), _fm(inp["norm_mix"][l]))
        put(("g2", l), _fm(inp["norm_ffn2"][l]))
        put(("bmix", l), _fm(inp["b_in_mix"][l]))
        put(("hgn", l), np.asarray(inp["hg_norm"][l], np.float32).reshape(128, 1))
        w = np.asarray(inp["lru_conv_w"][l], np.float32).reshape(4, 8, 128)
        put(("lcw", l), np.ascontiguousarray(w.transpose(2, 1, 0)).reshape(128, 32))
        put(("lcb", l), _fm(inp["lru_conv_b"][l]))
        gb = np.asarray(inp["lru_gate_b"][l], np.float32).reshape(2, 8, 128)
        put(("lgb", l), np.ascontiguousarray(gb.transpose(2, 0, 1)).reshape(128, 16))
        put(("lam", l), _fm(inp["lru_lambda"][l]))
        cw = np.asarray(inp["cv_dw_w"][l], np.float32).reshape(31, 8, 128)
        put(("cvw", l), np.ascontiguousarray(cw.transpose(2, 1, 0)).reshape(128, 248))
        put(("cvb", l), _fm(inp["cv_dw_b"][l]))
        put(("lng", l), _fm(inp["cv_ln_g"][l]))
        put(("lnb", l), _fm(inp["cv_ln_b"][l]))
    put("gf", _fm(inp["norm_final"]))
    lg = np.asarray(inp["hgrn_lb_logits"], np.float32).reshape(4, 8, 128)
    put("logits", np.ascontiguousarray(lg.transpose(2, 1, 0)).reshape(128, 32))
    lam = np.asarray(inp["lru_lambda"], np.float32).reshape(4, 8, 128)
    put("lam_all", np.ascontiguousarray(lam.transpose(2, 0, 1)).reshape(128, 32))
    return P


def blockify(w, kc):
    w = np.asarray(w, np.float32)
    n = w.shape[1] // 128
    return np.ascontiguousarray(w.reshape(kc, 128, n, 128).transpose(2, 1, 0, 3))


def pack_weights(inp, nl):
    out = {}
    win1, wout1, win2, wout2, wmix, wbr, wom, wgate, bbc = [], [], [], [], [], [], [], [], []
    for l in range(nl):
        win1.append(blockify(inp["ffn1_w_in"][l], 8))
        win2.append(blockify(inp["ffn2_w_in"][l], 8))
        for src, dst in ((inp["ffn1_w_out"][l], wout1), (inp["ffn2_w_out"][l], wout2)):
            b = blockify(src, NFF)
            bp = np.zeros((8, 128, 24, 128), np.float32)
            bp[:, :, :NFF, :] = b
            dst.append(np.ascontiguousarray(bp.reshape(8, 128, 3, 8, 128).transpose(0, 2, 1, 3, 4)))
        wmix.append(blockify(inp["w_in_mix"][l], 8))
        wbr.append(np.stack([blockify(inp["w_branch"][l][b], 8) for b in range(3)]))
        wom.append(blockify(inp["w_out_mix"][l], 8))
        g = np.asarray(inp["lru_gate_w"][l], np.float32)
        wgate.append(np.ascontiguousarray(g.transpose(2, 0, 1, 3)).reshape(128, 16, 128))
        hib = np.asarray(inp["b_in_mix"][l][2048:3072], np.float32)
        bbc.append(np.ascontiguousarray(np.broadcast_to(hib[None, :], (128, 1024))))
    out["win1"] = np.stack(win1)
    out["win2"] = np.stack(win2)
    out["wout1"] = np.stack(wout1)
    out["wout2"] = np.stack(wout2)
    out["wmix"] = np.stack(wmix)
    out["wbr"] = np.stack(wbr)
    out["wom"] = np.stack(wom)
    out["wgate"] = np.stack(wgate)
    out["bbc"] = np.stack(bbc)
    return out


def make_consts():
    c = np.zeros((128, 256), np.float32)
    c[:, 0:128] = np.eye(128, dtype=np.float32)
    s = np.arange(128)[:, None]
    t = np.arange(128)[None, :]
    c[:, 128:256] = (s <= t).astype(np.float32)
    return c


def build(T, TT, NL, phases=("ffn1", "mix", "ffn2"), final_norm=True, nw=10, nf=20, nb=12, dbg=None, NP=1, weave=True, KPOOL=0, GPOOL=False):
    NT = T // TT
    NCH = TT // 128
    nc = bass.Bass("TRN2", target_bir_lowering=False)

    def dram(name, shape, kind="ExternalInput"):
        return nc.dram_tensor(name, list(shape), F32, kind=kind).ap()

    d_x = dram("xT", [128, 8, NP * T])
    d_par = dram("params", [128, NPAR])
    d_con = dram("consts", [128, 256])
    d_win = [dram("win1", [NL, 2 * NFF, 128, 8, 128]), dram("win2", [NL, 2 * NFF, 128, 8, 128])]
    d_wout = [dram("wout1", [NL, 8, 3, 128, 8, 128]), dram("wout2", [NL, 8, 3, 128, 8, 128])]
    d_wmix = dram("wmix", [NL, NBLK, 128, 8, 128])
    d_wbr = dram("wbr", [NL, 3, 8, 128, 8, 128])
    d_wom = dram("wom", [NL, 8, 128, 8, 128])
    d_wgate = dram("wgate", [NL, 128, 16, 128])
    d_bbc = dram("bbc", [NL, 128, 1024])
    d_y = dram("yT", [128, 8, NP * T], kind="ExternalOutput")
    SW = 1024 + 8 + 24 + 240
    d_st = nc.dram_tensor("st_scratch", [max(NL, 1), 128, SW], F32, kind="Internal").ap()

    P = Prog(nc)
    nsem = [0]
    wspec = {"win1": (d_win[0], [2 * NFF, 128, 8, 128]), "wout1": (d_wout[0], [8, 3, 128, 8, 128]),
             "wmix": (d_wmix, [NBLK, 128, 8, 128]), "wbr": (d_wbr, [3, 8, 128, 8, 128]),
             "wom": (d_wom, [8, 128, 8, 128]), "win2": (d_win[1], [2 * NFF, 128, 8, 128]),
             "wout2": (d_wout[1], [8, 3, 128, 8, 128])}
    d_cvd = nc.dram_tensor("bf_cvd", [max(NL, 1), 8, 4, 128, 8, 128], BF16, kind="Internal").ap()
    d_stc = nc.dram_tensor("st_cub", [max(NL, 1), 128, 240], BF16, kind="Internal").ap()
    wbf = {}
    for nm, (src, shp) in wspec.items():
        wbf[nm] = nc.dram_tensor("bf_" + nm, [max(NL, 1)] + shp, BF16, kind="Internal").ap()

    def new_sem(name):
        nsem[0] += 1
        return nc.alloc_semaphore(name)

    def sb(name, shape, dt=F32):
        return Tile(nc.alloc_sbuf_tensor("sb_" + name, list(shape), dt)[:], name)

    def ps(name, shape, dt=F32):
        return Tile(nc.alloc_psum_tensor("ps_" + name, list(shape), dt)[:], name)

    sems = {e: new_sem("s_" + e) for e in ("pe", "act", "dve", "pool")}
    ds_par = DSem(new_sem("d_par"))
    ds_x = DSem(new_sem("d_x"))
    ds_y = DSem(new_sem("d_y"))
    ds_bbc = DSem(new_sem("d_bbc"))
    ds_gate = DSem(new_sem("d_gate"))
    ds_sts = DSem(new_sem("d_sts"))
    ds_stl = DSem(new_sem("d_stl"))
    st_t = [Tile(d_st[l], f"st{l}") for l in range(max(NL, 1))]

    xt = [sb(f"x{j}", [128, 8, TT]) for j in range(NT)]
    par = sb("par", [128, NPAR])
    npar = sb("npar", [128, NPAR])
    con = sb("con", [128, 256])
    ident = sb("ident", [128, 128], BF16)
    mask = con
    ones = sb("ones", [128, 128])
    smask = sb("smask", [128, TT])
    eps6 = sb("eps6", [128, 1])
    eps5 = sb("eps5", [128, 1])
    one1 = sb("one1", [128, 1])
    der = sb("der", [128, 4 * 8 * 4])
    uns = [sb("un0", [128, 8, TT], BF16), sb("un1", [128, 8, TT], BF16)]
    UN = [uns[0]]
    S_all = nc.alloc_sbuf_tensor("sb_S", [128, 8, 128], F32)[:]
    S = [Tile(S_all[:, h, :], f"S{h}") for h in range(8)]
    Sb = [sb(f"Sb{h}", [128, 128], BF16) for h in range(8)]
    hc_all = nc.alloc_sbuf_tensor("sb_hc", [128, 8], F32)[:]
    hc = [Tile(hc_all[:, h:h + 1], f"hc{h}") for h in range(8)]
    lxb_all = nc.alloc_sbuf_tensor("sb_lxb", [128, 8, 3 + TT], F32)[:]
    lxb = [Tile(lxb_all[:, h, :], f"lxb{h}") for h in range(8)]
    cub_all = nc.alloc_sbuf_tensor("sb_cub", [128, 8, 30 + TT], BF16)[:]
    cub = [Tile(cub_all[:, h, :], f"cub{h}") for h in range(8)]
    bbc = sb("bbc", [128, 1024])
    gw = sb("gw", [128, 16, 128], BF16)
    ybr = [sb(f"ybr{h}", [128, TT], BF16) for h in range(8)]
    mrg = [sb(f"mrg{m}", [128, TT]) for m in range(8)]
    mbf = [sb(f"mbf{m}", [128, TT], BF16) for m in range(8)]
    cv = [sb(f"cv{h}", [128, TT]) for h in range(8)]
    hT = [sb(f"hT{j}", [128, TT], BF16) for j in range(NFF)]
    ybrs = [ybr, hT[0:8], hT[8:16]]

    wslots = []
    for i in range(nw):
        t = sb(f"w{i}", [128, 8, 128], BF16)
        wslots.append((t, DSem(new_sem(f"d_w{i}"))))
    class Ctx:
        pass

    MAIN = Ctx()
    MAIN.WP = FreeList(wslots)
    MAIN.FP = FreeList([sb(f"f{i}", [128, TT]) for i in range(nf)])
    MAIN.BP = FreeList([sb(f"b{i}", [128, TT], BF16) for i in range(nb)])
    MAIN.PP = FreeList([ps(f"p{i}", [128, 512]) for i in range(7)])
    C = [MAIN]

    def sub_ctx(nfp, nbp, npp, nwp):
        c = Ctx()
        c.FP = FreeList([MAIN.FP.get() for _ in range(nfp)])
        c.BP = FreeList([MAIN.BP.get() for _ in range(nbp)])
        c.PP = FreeList([MAIN.PP.get() for _ in range(npp)])
        c.WP = FreeList([MAIN.WP.get() for _ in range(nwp)])
        return c

    def free_ctx(c):
        for name in ("FP", "BP", "PP", "WP"):
            fl = getattr(c, name)
            while fl.free:
                getattr(MAIN, name).put(fl.get())
    pT = ps("pT", [128, 1024], BF16)
    SCM = FreeList([sb(f"scm{i}", [128, TT], BF16) for i in range(2)])
    for t_ in list(SCM.free):
        P.memset("dve", t_.v, 0.0)

    def pcol(key, n=1, o=0):
        c = POFF[key] + o
        return par[:, c:c + n]

    def ncol(key, n=1, o=0):
        c = POFF[key] + o
        return npar[:, c:c + n]

    def sigmoid(out, in_, nbias=0.0, scale=1.0):
        P.act(out, in_, AF.Exp, bias=nbias, scale=-scale)
        P.act(out, out, AF.Ln, bias=one1.v)
        P.act(out, out, AF.Exp, scale=-1.0)

    wtile = {}
    wbf["cvd"] = d_cvd
    for l in range(NL):
        for nm, (src, shp) in wspec.items():
            if ("ffn1" not in phases and nm in ("win1", "wout1")) or ("ffn2" not in phases and nm in ("win2", "wout2")) \
                    or ("mix" not in phases and nm in ("wmix", "wbr", "wom")):
                continue
            dsw = DSem(new_sem(f"d_cv_{nm}{l}"))
            tl = Tile(wbf[nm][l], f"bf_{nm}{l}")
            wtile[(nm, l)] = tl
            nblk = 1
            for d_ in shp[:-3]:
                nblk *= d_
            srcf = src[l].rearrange("a b p k c -> (a b) p k c") if len(shp) == 5 else src[l]
            dstf = wbf[nm][l].rearrange("a b p k c -> (a b) p k c") if len(shp) == 5 else wbf[nm][l]
            for b0 in range(0, nblk, 8):
                b1 = min(nblk, b0 + 8)
                P.rec("pool", lambda e, o=dstf[b0:b1], i=srcf[b0:b1]: e.dma_start(out=o, in_=i), [], [tl], dma=True, dsem=dsw)
    P.dma("sp", par.v, d_par, ds_par)
    P.dma("sp", con.v, d_con, ds_par)
    P.ts("dve", npar.v, par.v, -1.0, None, ALU.mult)
    P.copy("dve", ident.v, con[:, 0:128])
    P.memset("dve", ones.v, 1.0)
    P.memset("dve", eps6.v, 1e-6)
    P.memset("dve", eps5.v, 1e-5)
    P.memset("dve", one1.v, 1.0)
    P.memset("dve", smask.v, 1.0)
    for c in range(NCH):
        P.memset("dve", smask[:, c * 128:c * 128 + 1], 0.0)

    def state_zero():
        P.rec("dve", lambda e: e.memset(S_all, 0.0), [], S)
        P.rec("dve", lambda e: e.memset(hc_all, 0.0), [], hc)
        P.rec("dve", lambda e: e.memset(lxb_all[:, :, 0:3], 0.0), [], lxb)
        P.rec("dve", lambda e: e.memset(cub_all[:, :, 0:30], 0.0), [], cub)
        for h in range(8):
            P.memset("dve", Sb[h].v, 0.0)

    def state_save(l):
        d = d_st[l]
        P.rec("sp", lambda e: e.dma_start(out=d[:, 0:1024].rearrange("p (h v) -> p h v", v=128), in_=S_all),
              S, [st_t[l]], dma=True, dsem=ds_sts)
        P.rec("sp", lambda e: e.dma_start(out=d[:, 1024:1032], in_=hc_all), hc, [st_t[l]], dma=True, dsem=ds_sts)
        P.rec("sp", lambda e: e.dma_start(out=d[:, 1032:1056].rearrange("p (h j) -> p h j", j=3), in_=lxb_all[:, :, 0:3]),
              lxb, [st_t[l]], dma=True, dsem=ds_sts)
        P.rec("sp", lambda e: e.dma_start(out=d_stc[l].rearrange("p (h j) -> p h j", j=30), in_=cub_all[:, :, 0:30]),
              cub, [st_t[l]], dma=True, dsem=ds_sts)

    def state_load(l):
        d = d_st[l]
        P.rec("sp", lambda e: e.dma_start(out=S_all, in_=d[:, 0:1024].rearrange("p (h v) -> p h v", v=128)),
              [st_t[l]], S, dma=True, dsem=ds_stl)
        P.rec("sp", lambda e: e.dma_start(out=hc_all, in_=d[:, 1024:1032]), [st_t[l]], hc, dma=True, dsem=ds_stl)
        P.rec("sp", lambda e: e.dma_start(out=lxb_all[:, :, 0:3], in_=d[:, 1032:1056].rearrange("p (h j) -> p h j", j=3)),
              [st_t[l]], lxb, dma=True, dsem=ds_stl)
        P.rec("sp", lambda e: e.dma_start(out=cub_all[:, :, 0:30], in_=d_stc[l].rearrange("p (h j) -> p h j", j=30)),
              [st_t[l]], cub, dma=True, dsem=ds_stl)
        for h in range(8):
            P.copy("act", Sb[h].v, S[h].v)

    def dcol(kind, l, c=0, n=8):
        o = kind * 32 + l * 8 + c
        return der[:, o:o + n]

    ft = C[0].FP.get()
    ex = ft[:, 0:32]
    P.act(ex, pcol("logits", 32), AF.Exp)
    sm = ft[:, 32:40]
    ex3 = TV(ft, ft.ap[:, 0:32].rearrange("p (c l) -> p c l", l=4))
    P.reduce_sum("dve", sm, ex3)
    rc = ft[:, 40:48]
    P.recip(rc, sm)
    P.memset("dve", dcol(0, 0), 0.0)
    for l in range(1, 4):
        exl = TV(ft, ft.ap[:, 0:32].rearrange("p (c l) -> p l c", l=4)[:, l, :])
        P.tt("dve", ft[:, 48:56], exl, rc, ALU.mult)
        P.tt("dve", dcol(0, l), dcol(0, l - 1), ft[:, 48:56], ALU.add)
    P.ts("dve", der[:, 32:64], der[:, 0:32], -1.0, 1.0, ALU.mult, ALU.add)
    P.ts("dve", der[:, 64:96], der[:, 0:32], 1.0, -1.0, ALU.mult, ALU.add)
    P.act(ft[:, 64:96], pcol("lam_all", 32), AF.Exp, scale=-1.0)
    P.act(ft[:, 64:96], ft[:, 64:96], AF.Ln, bias=one1.v)
    P.ts("dve", der[:, 96:128], ft[:, 64:96], -8.0, None, ALU.mult)
    C[0].FP.put(ft)

    if "mix" in phases:
        for l in range(NL):
            dsd = DSem(new_sem(f"d_cvd{l}"))
            tl = Tile(d_cvd[l], f"cvd{l}")
            wtile[("cvd", l)] = tl
            for h in range(8):
                for g in range(4):
                    stg, dsg = MAIN.WP.get()
                    for jj in range(8):
                        jx = g * 8 + jj
                        if jx < 31:
                            wcol = POFF[("cvw", l)] + h * 31 + jx
                            P.act(stg[:, jj, :], ident.v, AF.Copy, scale=par[:, wcol:wcol + 1])
                        else:
                            P.memset("dve", stg[:, jj, :], 0.0)
                    P.rec("sp", lambda e, o=d_cvd[l, h, g], i=stg.ap: e.dma_start(out=o, in_=i), [stg], [tl], dma=True, dsem=dsd)
                    MAIN.WP.put((stg, dsg))

    def load_w(src):
        nm, l, idx = src
        ap = wbf[nm][l]
        for i_ in idx:
            ap = ap[i_]
        t, ds = C[0].WP.get()
        P.rec("sp", lambda e, o=t.ap, i=ap: e.dma_start(out=o, in_=i), [wtile[(nm, l)]], [t], dma=True, dsem=ds)
        return t, ds

    def rmsnorm_to_un(j, gkey, un):
        pss = C[0].PP.get()
        for c in range(8):
            sq = C[0].FP.get()
            P.act(sq.v, xt[j][:, c, :], AF.Square)
            P.mm(pss[:, 0:TT], ones.v, sq.v, c == 0, c == 7)
            C[0].FP.put(sq)
        rs = C[0].FP.get()
        P.act(rs.v, pss[:, 0:TT], AF.Ln, bias=eps6.v, scale=1.0 / D)
        C[0].PP.put(pss)
        P.act(rs.v, rs.v, AF.Exp, scale=-0.5)
        for c in range(8):
            P.stt("dve", un[:, c, :], xt[j][:, c, :], pcol(gkey, 1, c), rs.v, ALU.mult, ALU.mult)
        C[0].FP.put(rs)

    def proj(src):
        w, ds = load_w(src)
        p = C[0].PP.get()
        for c in range(8):
            P.mm(p[:, 0:TT], w[:, c, :], UN[0][:, c, :], c == 0, c == 7)
        C[0].WP.put((w, ds))
        return p

    def ffn(l, which, j):
        dwin = "win1" if which == 0 else "win2"
        dwout = "wout1" if which == 0 else "wout2"
        for jj in range(NFF):
            pg = proj((dwin, l, (jj,)))
            pu = proj((dwin, l, (NFF + jj,)))
            sg = C[0].FP.get()
            sigmoid(sg.v, pg[:, 0:TT])
            P.tt("dve", sg.v, sg.v, pg[:, 0:TT], ALU.mult)
            C[0].PP.put(pg)
            P.tt("dve", hT[jj].v, sg.v, pu[:, 0:TT], ALU.mult)
            C[0].FP.put(sg)
            C[0].PP.put(pu)
        for m in range(8):
            py = C[0].PP.get()
            for piece in range(3):
                w, ds = load_w((dwout, l, (m, piece)))
                nk = 8 if piece < 2 else NFF - 16
                for k in range(nk):
                    kk = piece * 8 + k
                    P.mm(py[:, 0:TT], w[:, k, :], hT[kk].v, kk == 0, kk == NFF - 1)
                C[0].WP.put((w, ds))
            P.stt("dve", xt[j][:, m, :], py[:, 0:TT], 0.5, xt[j][:, m, :], ALU.mult, ALU.add)
            C[0].PP.put(py)

    def branch_project(l, b, first, last):
        for m in range(8):
            w, ds = load_w(("wbr", l, (b, m)))
            pb = C[0].PP.get()
            for k in range(8):
                P.mm(pb[:, 0:TT], w[:, k, :], ybrs[b][k].v, k == 0, k == 7)
            C[0].WP.put((w, ds))
            blk = 64 + 8 * b + m
            pgt = proj(("wmix", l, (blk,)))
            gt = C[0].FP.get()
            sigmoid(gt.v, pgt[:, 0:TT], ncol(("bmix", l), 1, blk))
            C[0].PP.put(pgt)
            if first:
                P.tt("dve", mrg[m].v, pb[:, 0:TT], gt.v, ALU.mult)
            else:
                P.tt("dve", gt.v, pb[:, 0:TT], gt.v, ALU.mult)
                if last:
                    P.tt("dve", mbf[m].v, mrg[m].v, gt.v, ALU.add)
                else:
                    P.tt("dve", mrg[m].v, mrg[m].v, gt.v, ALU.add)
            C[0].PP.put(pb)
            C[0].FP.put(gt)

    def hgrn_head(l, h, j):
        bm = ("bmix", l)
        lb = dcol(0, l, h, 1)
        oml = dcol(1, l, h, 1)
        noml = dcol(2, l, h, 1)
        pz = proj(("wmix", l, (8 + h,)))
        sig = C[0].FP.get()
        sigmoid(sig.v, pz[:, 0:TT], ncol(bm, 1, 8 + h))
        C[0].PP.put(pz)
        lf = C[0].FP.get()
        P.ts("dve", lf.v, sig.v, oml, lb, ALU.mult, ALU.add)
        P.act(lf.v, lf.v, AF.Ln)
        kk = C[0].FP.get()
        P.ts("dve", kk.v, sig.v, noml, oml, ALU.mult, ALU.add)
        C[0].FP.put(sig)
        bcum = C[0].FP.get()
        P.scan("dve", bcum.v, smask.v, lf.v, 0.0, ALU.mult, ALU.add)
        C[0].FP.put(lf)
        b3 = TV(bcum, bcum.ap.rearrange("p (c t) -> p c t", t=128))
        e1 = C[0].FP.get()
        e13 = TV(e1, e1.ap.rearrange("p (c t) -> p c t", t=128))
        P.tt("dve", e13, b3, b3[:, :, 63:64].bc([128, NCH, 128]), ALU.subtract)
        e2 = C[0].FP.get()
        P.act(e2.v, e1.v, AF.Exp, scale=-1.0)
        P.act(e1.v, e1.v, AF.Exp)
        e4 = C[0].FP.get()
        e43 = TV(e4, e4.ap.rearrange("p (c t) -> p c t", t=128))
        P.tt("dve", e43, b3[:, :, 127:128].bc([128, NCH, 128]), b3, ALU.subtract)
        P.act(e4.v, e4.v, AF.Exp)
        P.act(bcum.v, bcum.v, AF.Exp)
        e3 = bcum
        pq = proj(("wmix", l, (h,)))
        q = C[0].FP.get()
        sigmoid(q.v, pq[:, 0:TT], ncol(bm, 1, h))
        P.stt("dve", q.v, pq[:, 0:TT], pcol(bm, 1, h), q.v, ALU.add, ALU.mult)
        C[0].PP.put(pq)
        qt = C[0].BP.get(); kt = C[0].BP.get(); qh = C[0].BP.get(); kh = C[0].BP.get()
        P.tt("dve", qt.v, q.v, e1.v, ALU.mult)
        P.tt("dve", kt.v, kk.v, e2.v, ALU.mult)
        P.tt("dve", qh.v, q.v, e3.v, ALU.mult)
        P.tt("dve", kh.v, kk.v, e4.v, ALU.mult)
        C[0].FP.put(q); C[0].FP.put(kk); C[0].FP.put(e1); C[0].FP.put(e2); C[0].FP.put(e4)
        w, ds = load_w(("wmix", l, (16 + h,)))
        pv = C[0].PP.get()
        for c in range(NCH):
            for k in range(8):
                P.mm(pv[:, c * 128:(c + 1) * 128], UN[0][:, k, c * 128:(c + 1) * 128], w[:, k, :], k == 0, k == 7)
        C[0].WP.put((w, ds))
        V = C[0].BP.get()
        pv3 = TV(pv, pv.ap[:, 0:TT].rearrange("p (c v) -> p c v", v=128))
        V3 = TV(V, V.ap.rearrange("p (c v) -> p c v", v=128))
        P.tt("dve", V3, pv3,
             TV(bbc, bbc.ap[:, h * 128:(h + 1) * 128].rearrange("p (o v) -> p o v", o=1).to_broadcast([128, NCH, 128])),
             ALU.add)
        C[0].PP.put(pv)
        for c in range(NCH):
            P.transpose(pT[:, c * 128:(c + 1) * 128], kh[:, c * 128:(c + 1) * 128], ident.v)
        khT = C[0].BP.get()
        P.copy("dve", khT.v, pT[:, 0:TT])
        C[0].BP.put(kh)
        psc = C[0].PP.get()
        for c in range(NCH):
            P.mm(psc[:, c * 128:(c + 1) * 128], kt[:, c * 128:(c + 1) * 128], qt[:, c * 128:(c + 1) * 128], True, True)
        scm = SCM.get()
        mk = TV(con, con.ap[:, 128:256].bitcast(mybir.dt.uint32).rearrange("p (o t) -> p o t", o=1).to_broadcast([128, NCH, 128]))
        so = TV(scm, scm.ap.rearrange("p (c t) -> p c t", t=128))
        si = TV(psc, psc.ap[:, 0:TT].rearrange("p (c t) -> p c t", t=128))
        P.rec("dve", lambda e, o=so.ap, m=mk.ap, d=si.ap: e.copy_predicated(o, m, d), [con, psc], [scm])
        C[0].PP.put(psc)
        C[0].BP.put(qt); C[0].BP.put(kt)
        po = C[0].PP.get()
        for c in range(NCH):
            cs = slice(c * 128, (c + 1) * 128)
            P.mm(po[:, cs], V[:, cs], scm[:, cs], True, False)
            P.mm(po[:, cs], Sb[h].v, qh[:, cs], False, True)
            pS = C[0].PP.get()
            P.mm(pS[:, 0:128], khT[:, cs], V[:, cs], True, True)
            P.stt("dve", S[h].v, S[h].v, e3[:, c * 128 + 127:c * 128 + 128], pS[:, 0:128], ALU.mult, ALU.add)
            C[0].PP.put(pS)
            P.copy("act", Sb[h].v, S[h].v)
        C[0].BP.put(V); SCM.put(scm); C[0].BP.put(qh); C[0].BP.put(khT)
        C[0].FP.put(e3)
        if dbg == "hg0":
            P.copy("act", ybrs[0][h].v, po[:, 0:TT])
            C[0].PP.put(po)
            return
        osq = C[0].FP.get()
        P.act(osq.v, po[:, 0:TT], AF.Square)
        pss = C[0].PP.get()
        P.mm(pss[:, 0:TT], ones.v, osq.v, True, True)
        rs = osq
        P.act(rs.v, pss[:, 0:TT], AF.Ln, bias=eps6.v, scale=1.0 / 128)
        C[0].PP.put(pss)
        P.act(rs.v, rs.v, AF.Exp, scale=-0.5)
        pg = proj(("wmix", l, (24 + h,)))
        sg = C[0].FP.get()
        sigmoid(sg.v, pg[:, 0:TT], ncol(bm, 1, 24 + h))
        P.stt("dve", sg.v, pg[:, 0:TT], pcol(bm, 1, 24 + h), sg.v, ALU.add, ALU.mult)
        C[0].PP.put(pg)
        P.stt("dve", rs.v, po[:, 0:TT], pcol(("hgn", l)), rs.v, ALU.mult, ALU.mult)
        C[0].PP.put(po)
        P.tt("dve", ybrs[0][h].v, rs.v, sg.v, ALU.mult)
        C[0].FP.put(rs); C[0].FP.put(sg)

    def lru_block(l, h, j):
        bm = ("bmix", l)
        plx = proj(("wmix", l, (32 + h,)))
        P.ts("dve", lxb[h][:, 3:3 + TT], plx[:, 0:TT], pcol(bm, 1, 32 + h), None, ALU.add)
        C[0].PP.put(plx)
        xb = C[0].FP.get()
        wc = POFF[("lcw", l)] + h * 4
        P.ts("dve", xb.v, lxb[h][:, 3:3 + TT], par[:, wc + 3:wc + 4], pcol(("lcb", l), 1, h), ALU.mult, ALU.add)
        for jx in (2, 1, 0):
            P.stt("dve", xb.v, lxb[h][:, jx:jx + TT], par[:, wc + jx:wc + jx + 1], xb.v, ALU.mult, ALU.add)
        P.copy("dve", lxb[h][:, 0:3], lxb[h][:, TT:TT + 3])
        xbb = C[0].BP.get()
        P.copy("dve", xbb.v, xb.v)
        pr = C[0].PP.get()
        P.mm(pr[:, 0:TT], gw[:, h, :], xbb.v, True, True)
        pi = C[0].PP.get()
        P.mm(pi[:, 0:TT], gw[:, 8 + h, :], xbb.v, True, True)
        C[0].BP.put(xbb)
        a = C[0].FP.get()
        sigmoid(a.v, pr[:, 0:TT], ncol(("lgb", l), 1, h))
        C[0].PP.put(pr)
        it = C[0].FP.get()
        sigmoid(it.v, pi[:, 0:TT], ncol(("lgb", l), 1, 8 + h))
        C[0].PP.put(pi)
        P.act(a.v, a.v, AF.Exp, scale=dcol(3, l, h, 1))
        mt = C[0].FP.get()
        P.tt("dve", mt.v, a.v, a.v, ALU.mult)
        P.act(mt.v, mt.v, AF.Ln, bias=one1.v, scale=-1.0)
        P.act(mt.v, mt.v, AF.Exp, scale=0.5)
        P.tt("dve", it.v, it.v, xb.v, ALU.mult)
        P.tt("dve", it.v, it.v, mt.v, ALU.mult)
        C[0].FP.put(xb)
        hs = mt
        P.scan("dve", hs.v, a.v, it.v, hc[h].v, ALU.mult, ALU.add)
        P.copy("dve", hc[h].v, hs[:, TT - 1:TT])
        C[0].FP.put(a); C[0].FP.put(it)
        plg = proj(("wmix", l, (40 + h,)))
        lg = C[0].FP.get()
        P.ts("dve", lg.v, plg[:, 0:TT], pcol(bm, 1, 40 + h), None, ALU.add)
        C[0].PP.put(plg)
        t = C[0].FP.get()
        ge = "pool" if GPOOL else "dve"
        P.tt(ge, t.v, lg.v, lg.v, ALU.mult)
        P.ts(ge, t.v, t.v, 0.044715, 1.0, ALU.mult, ALU.add)
        P.tt(ge, t.v, t.v, lg.v, ALU.mult)
        sigmoid(t.v, t.v, 0.0, 1.5957691216057308)
        P.tt(ge, t.v, t.v, lg.v, ALU.mult)
        P.tt("dve", ybrs[1][h].v, t.v, hs.v, ALU.mult)
        C[0].FP.put(lg); C[0].FP.put(t); C[0].FP.put(hs)

    def conv_block(l, h, j):
        bm = ("bmix", l)
        pca = proj(("wmix", l, (48 + h,)))
        pcb = proj(("wmix", l, (56 + h,)))
        sg = C[0].FP.get()
        sigmoid(sg.v, pcb[:, 0:TT], ncol(bm, 1, 56 + h))
        C[0].PP.put(pcb)
        P.stt("dve", cub[h][:, 30:30 + TT], pca[:, 0:TT], pcol(bm, 1, 48 + h), sg.v, ALU.add, ALU.mult)
        C[0].PP.put(pca)
        C[0].FP.put(sg)
        pcv = C[0].PP.get()
        for g in range(4):
            w, ds = load_w(("cvd", l, (h, g)))
            for jj in range(8):
                jx = g * 8 + jj
                if jx < 31:
                    P.mm(pcv[:, 0:TT], w[:, jj, :], cub[h][:, jx:jx + TT], jx == 0, jx == 30)
            C[0].WP.put((w, ds))
        P.act(cv[h].v, pcv[:, 0:TT], AF.Identity, bias=pcol(("cvb", l), 1, h))
        C[0].PP.put(pcv)
        P.copy("dve", cub[h][:, 0:30], cub[h][:, TT:TT + 30])

    def conv_finish(l, j):
        import os
        stage = int(os.environ.get("DBGSTAGE", "9"))
        pm = C[0].PP.get()
        pq = C[0].PP.get()
        for h in range(8):
            P.mm(pm[:, 0:TT], ones.v, cv[h].v, h == 0, h == 7)
        for h in range(8):
            sq = C[0].FP.get()
            P.act(sq.v, cv[h].v, AF.Square)
            P.mm(pq[:, 0:TT], ones.v, sq.v, h == 0, h == 7)
            C[0].FP.put(sq)
        mu = C[0].FP.get()
        P.ts("dve", mu.v, pm[:, 0:TT], 1.0 / D, None, ALU.mult)
        C[0].PP.put(pm)
        rs = C[0].FP.get()
        P.tt("dve", rs.v, mu.v, mu.v, ALU.mult)
        if stage >= 2:
            P.stt("dve", rs.v, pq[:, 0:TT], 1.0 / D, rs.v, ALU.mult, ALU.subtract)
        C[0].PP.put(pq)
        if stage >= 3:
            P.act(rs.v, rs.v, AF.Ln, bias=eps5.v)
            P.act(rs.v, rs.v, AF.Exp, scale=-0.5)
        for h in range(8):
            t = C[0].FP.get()
            P.tt("dve", t.v, cv[h].v, mu.v, ALU.subtract)
            P.stt("dve", t.v, t.v, pcol(("lng", l), 1, h), rs.v, ALU.mult, ALU.mult)
            sg = C[0].FP.get()
            sigmoid(sg.v, t.v, ncol(("lnb", l), 1, h))
            P.stt("dve", ybrs[2][h].v, t.v, pcol(("lnb", l), 1, h), sg.v, ALU.add, ALU.mult)
            C[0].FP.put(sg)
            C[0].FP.put(t)
        C[0].FP.put(mu); C[0].FP.put(rs)

    def mixer(l, j, p=0, nxt=None):
        if dbg is not None:
            for h in range(8):
                {"hg": hgrn_head, "hg0": hgrn_head, "lru": lru_block, "cv": conv_block, "cv0": conv_block}[dbg](l, h, j)
            if dbg == "cv":
                conv_finish(l, j)
            for h in range(8):
                t = C[0].FP.get()
                P.copy("dve", t.v, cv[h].v if dbg == "cv0" else ybrs[{"hg": 0, "hg0": 0, "lru": 1, "cv": 2}[dbg]][h].v)
                P.dma("sp", d_y[:, h, p * T + j * TT:p * T + (j + 1) * TT], t.v, ds_y)
                C[0].FP.put(t)
            return
        if weave:
            ctxs = [sub_ctx(9, 7, 3, 4), sub_ctx(7, 2, 2, 3), sub_ctx(4, 0, 2, 3)]
            for h in range(8):
                sts = []
                for fn_, cx in ((hgrn_head, ctxs[0]), (lru_block, ctxs[1]), (conv_block, ctxs[2])):
                    C[0] = cx
                    P.begin_stream()
                    fn_(l, h, j)
                    sts.append(P.end_stream())
                C[0] = MAIN
                P.merge(sts)
            for cx in ctxs:
                free_ctx(cx)
        else:
            for h in range(8):
                hgrn_head(l, h, j)
            for h in range(8):
                lru_block(l, h, j)
            for h in range(8):
                conv_block(l, h, j)
        ctxr = sub_ctx(3, 0, 1, 0) if nxt is not None else None
        if nxt is not None:
            P.begin_stream()
        branch_project(l, 0, True, False)
        branch_project(l, 1, False, False)
        conv_finish(l, j)
        branch_project(l, 2, False, True)
        for m in range(8):
            w, ds = load_w(("wom", l, (m,)))
            pw = C[0].PP.get()
            for k in range(8):
                P.mm(pw[:, 0:TT], w[:, k, :], mbf[k].v, k == 0, k == 7)
            C[0].WP.put((w, ds))
            P.tt("dve", xt[j][:, m, :], pw[:, 0:TT], xt[j][:, m, :], ALU.add)
            C[0].PP.put(pw)
        if nxt is not None:
            sa = P.end_stream()
            C[0] = ctxr
            P.begin_stream()
            rmsnorm_to_un(nxt[0], nxt[1], nxt[2])
            sb_ = P.end_stream()
            C[0] = MAIN
            P.merge([sa, sb_])
            free_ctx(ctxr)

    def ffn_woven(l, which, j, nxt):
        if nxt is None:
            ffn(l, which, j)
            return
        ctxr = sub_ctx(3, 0, 1, 0)
        P.begin_stream()
        ffn(l, which, j)
        sa = P.end_stream()
        C[0] = ctxr
        P.begin_stream()
        rmsnorm_to_un(nxt[0], nxt[1], nxt[2])
        sb_ = P.end_stream()
        C[0] = MAIN
        P.merge([sa, sb_])
        free_ctx(ctxr)

    for p in range(NP):
        for j in range(NT):
            P.dma("sp", xt[j].v, d_x[:, :, p * T + j * TT:p * T + (j + 1) * TT], ds_x)
        seq = []
        for l in range(NL):
            for kind in ("ffn1", "mix", "ffn2"):
                if kind in phases:
                    for j in range(NT):
                        seq.append((kind, l, j))
        gk = {"ffn1": "g1", "mix": "gm", "ffn2": "g2"}
        for i, (kind, l, j) in enumerate(seq):
            if i == 0:
                rmsnorm_to_un(j, (gk[kind], l), uns[0])
            UN[0] = uns[i % 2]
            nxt = None
            if i + 1 < len(seq) and NT >= 2 and dbg is None:
                k2, l2, j2 = seq[i + 1]
                nxt = (j2, (gk[k2], l2), uns[(i + 1) % 2])
            elif i + 1 < len(seq):
                k2, l2, j2 = seq[i + 1]
                nxt = None
            if kind == "mix" and j == 0:
                P.dma("sp", bbc.v, d_bbc[l], ds_bbc)
                P.dma("pool", gw.v, d_wgate[l], ds_gate)
                if p == 0:
                    state_zero()
                else:
                    state_load(l)
            if kind == "mix":
                mixer(l, j, p, nxt)
                if j == NT - 1 and p < NP - 1:
                    state_save(l)
            else:
                ffn_woven(l, 0 if kind == "ffn1" else 1, j, nxt)
            if nxt is None and i + 1 < len(seq):
                k2, l2, j2 = seq[i + 1]
                rmsnorm_to_un(j2, (gk[k2], l2), uns[(i + 1) % 2])
        for j in range(NT):
            if final_norm:
                pss = C[0].PP.get()
                for c in range(8):
                    sq = C[0].FP.get()
                    P.act(sq.v, xt[j][:, c, :], AF.Square)
                    P.mm(pss[:, 0:TT], ones.v, sq.v, c == 0, c == 7)
                    C[0].FP.put(sq)
                rs = C[0].FP.get()
                P.act(rs.v, pss[:, 0:TT], AF.Ln, bias=eps6.v, scale=1.0 / D)
                C[0].PP.put(pss)
                P.act(rs.v, rs.v, AF.Exp, scale=-0.5)
                for c in range(8):
                    P.stt("dve", xt[j][:, c, :], xt[j][:, c, :], pcol("gf", 1, c), rs.v, ALU.mult, ALU.mult)
                C[0].FP.put(rs)
            if dbg is None:
                P.dma("sp", d_y[:, :, p * T + j * TT:p * T + (j + 1) * TT], xt[j].v, ds_y)

    with nc.Block() as block:
        P.emit(block, sems, [ds_y])
    return nc, P


N_CORES = 4
T_PASS = 2048
N_PASS = 2
TT_CORE = 256


def make_in_maps(inp, xs, nl):
    wts = pack_weights(inp, nl)
    par = pack_params(inp)
    con = make_consts()
    maps = []
    for xc in xs:
        T = xc.shape[0]
        xT = np.ascontiguousarray(np.asarray(xc, np.float32).reshape(T, 8, 128).transpose(2, 1, 0))
        m = {"xT": xT, "params": par, "consts": con}
        m.update(wts)
        maps.append(m)
    return maps


def kernel(**inp):
    x = np.asarray(inp["x"], np.float32)
    B, S, _ = x.shape
    xs = [x[b] for b in range(B)]
    nc, _ = build(T_PASS, TT_CORE, DEPTH, NP=N_PASS)
    in_maps = make_in_maps(inp, xs, DEPTH)
    res = run_bass_kernel_spmd(nc, in_maps, core_ids=list(range(len(xs))))
    outs = []
    for r in res.results:
        yT = np.asarray(r["yT"], np.float32)
        outs.append(yT.transpose(2, 1, 0).reshape(S, D))
    return np.stack(outs).astype(np.float32)
```

```python
import numpy as np
from collections import deque

import concourse.bass as bass
import concourse.mybir as mybir
from concourse.bass_utils import run_bass_kernel_spmd

F32 = mybir.dt.float32
BF16 = mybir.dt.bfloat16
AF = mybir.ActivationFunctionType
ALU = mybir.AluOpType

D = 1024
DFF = 2816
NFF = DFF // 128
MIXC = 11264
NBLK = MIXC // 128
DEPTH = 4
RAW_WINDOW = 6


class Tile:
    __slots__ = ("ap", "last_w", "readers", "name")

    def __init__(self, ap, name=""):
        self.ap = ap
        self.last_w = None
        self.readers = {}
        self.name = name

    def __getitem__(self, k):
        return TV(self, self.ap[k])

    @property
    def v(self):
        return TV(self, self.ap)


class TV:
    __slots__ = ("tile", "ap")

    def __init__(self, tile, ap):
        self.tile = tile
        self.ap = ap

    def __getitem__(self, k):
        return TV(self.tile, self.ap[k])

    def bc(self, shape):
        return TV(self.tile, self.ap.to_broadcast(list(shape)))


class DSem:
    def __init__(self, sem):
        self.sem = sem
        self.count = 0


class Op:
    __slots__ = ("eng", "fn", "deps", "dma", "dsem", "done", "signal", "sig", "pos", "dwait", "sid")

    def __init__(self, eng, fn, dma=False, dsem=None):
        self.eng = eng
        self.fn = fn
        self.dma = dma
        self.dsem = dsem
        self.deps = []
        self.dwait = []
        self.done = 0
        self.signal = False
        self.sig = 0
        self.pos = 0
        self.sid = 0


class FreeList:
    def __init__(self, items):
        self.free = deque(items)

    def get(self):
        if not self.free:
            raise RuntimeError("pool exhausted")
        return self.free.popleft()

    def put(self, t):
        self.free.append(t)


def _ap(a):
    return a.ap if isinstance(a, TV) else a


def _tiles(*args):
    out = []
    for a in args:
        if isinstance(a, TV):
            out.append(a.tile)
    return out


class Prog:
    ENGS = ("pe", "act", "dve", "pool", "sp")

    def __init__(self, nc):
        self.nc = nc
        self.ops = {e: [] for e in self.ENGS}
        self.nops = 0
        self.cur = None
        self.sid = 0
        self.nsid = 0
        self.spos = {}

    def rec(self, eng, fn, reads, writes, dma=False, dsem=None):
        op = Op(eng, fn, dma, dsem)
        op.sid = self.sid
        if self.cur is None:
            op.pos = len(self.ops[eng])
        else:
            op.pos = self.spos.get(eng, 0)
            self.spos[eng] = op.pos + 1
        deps = {}
        raw = set()
        for t in reads:
            if t.last_w is not None:
                deps[id(t.last_w)] = t.last_w
                raw.add(id(t.last_w))
        for t in writes:
            if t.last_w is not None:
                deps[id(t.last_w)] = t.last_w
            for r in t.readers.values():
                deps[id(r)] = r
        for d in deps.values():
            if d is op:
                continue
            if d.dma:
                op.dwait.append((d.dsem, d.dsem.count))
                continue
            if (not dma) and d.eng == eng:
                if eng == "pe":
                    continue
                if id(d) not in raw or (d.sid == op.sid and op.pos - d.pos > RAW_WINDOW):
                    continue
            d.signal = True
            op.deps.append(d)
        if dma:
            dsem.count += 16
            op.done = dsem.count
        for t in writes:
            t.last_w = op
            t.readers = {}
        wset = set(id(t) for t in writes)
        for t in reads:
            if id(t) not in wset:
                key = id(dsem) if dma else eng
                t.readers[key] = op
        if self.cur is None:
            self.ops[eng].append(op)
        else:
            self.cur.append(op)
        self.nops += 1
        return op

    def begin_stream(self):
        self.nsid += 1
        self.sid = self.nsid
        self.cur = []
        self.spos = {}

    def end_stream(self):
        st = self.cur
        self.cur = None
        self.sid = 0
        return st

    def merge(self, streams):
        streams = [st for st in streams if st]
        idx = [0] * len(streams)
        while True:
            best = -1
            bf = 2.0
            for i, st in enumerate(streams):
                if idx[i] < len(st):
                    f = idx[i] / len(st)
                    if f < bf:
                        bf = f
                        best = i
            if best < 0:
                break
            op = streams[best][idx[best]]
            idx[best] += 1
            self.ops[op.eng].append(op)

    def mm(self, out, lhsT, rhs, start, stop):
        o, l, r = _ap(out), _ap(lhsT), _ap(rhs)
        return self.rec("pe", lambda e: e.matmul(o, l, r, start=start, stop=stop),
                        _tiles(lhsT, rhs), _tiles(out))

    def transpose(self, out, in_, ident):
        o, i, d = _ap(out), _ap(in_), _ap(ident)
        return self.rec("pe", lambda e: e.transpose(o, i, d), _tiles(in_, ident), _tiles(out))

    def act(self, out, in_, func, bias=0.0, scale=1.0):
        o, i, b, s = _ap(out), _ap(in_), _ap(bias), _ap(scale)
        return self.rec("act", lambda e: e.activation(out=o, in_=i, func=func, bias=b, scale=s),
                        _tiles(in_, bias, scale), _tiles(out))

    def tt(self, eng, out, in0, in1, op):
        o, a, b = _ap(out), _ap(in0), _ap(in1)
        return self.rec(eng, lambda e: e.tensor_tensor(out=o, in0=a, in1=b, op=op),
                        _tiles(in0, in1), _tiles(out))

    def ts(self, eng, out, in0, s1, s2, op0, op1=None):
        o, a, x1, x2 = _ap(out), _ap(in0), _ap(s1), _ap(s2)
        if op1 is None:
            fn = lambda e: e.tensor_scalar(out=o, in0=a, scalar1=x1, scalar2=None, op0=op0)
        else:
            fn = lambda e: e.tensor_scalar(out=o, in0=a, scalar1=x1, scalar2=x2, op0=op0, op1=op1)
        return self.rec(eng, fn, _tiles(in0, s1, s2), _tiles(out))

    def stt(self, eng, out, in0, scalar, in1, op0, op1):
        o, a, s, b = _ap(out), _ap(in0), _ap(scalar), _ap(in1)
        return self.rec(eng, lambda e: e.scalar_tensor_tensor(out=o, in0=a, scalar=s, in1=b, op0=op0, op1=op1),
                        _tiles(in0, scalar, in1), _tiles(out))

    def scan(self, eng, out, d0, d1, init, op0, op1):
        o, a, b, i = _ap(out), _ap(d0), _ap(d1), _ap(init)
        return self.rec(eng, lambda e: e.tensor_tensor_scan(out=o, data0=a, data1=b, initial=i, op0=op0, op1=op1),
                        _tiles(d0, d1, init), _tiles(out))

    def copy(self, eng, out, in_):
        o, i = _ap(out), _ap(in_)
        if eng == "act":
            return self.rec("act", lambda e: e.activation(out=o, in_=i, func=AF.Copy), _tiles(in_), _tiles(out))
        return self.rec(eng, lambda e: e.tensor_copy(out=o, in_=i), _tiles(in_), _tiles(out))

    def memset(self, eng, out, val):
        o = _ap(out)
        return self.rec(eng, lambda e: e.memset(o, val), [], _tiles(out))

    def reduce_sum(self, eng, out, in_):
        o, i = _ap(out), _ap(in_)
        return self.rec(eng, lambda e: e.reduce_sum(out=o, in_=i, axis=mybir.AxisListType.X), _tiles(in_), _tiles(out))

    def recip(self, out, in_):
        o, i = _ap(out), _ap(in_)
        return self.rec("dve", lambda e: e.reciprocal(out=o, in_=i), _tiles(in_), _tiles(out))

    def dma(self, q, out, in_, dsem):
        o, i = _ap(out), _ap(in_)
        return self.rec(q, lambda e: e.dma_start(out=o, in_=i), _tiles(in_), _tiles(out), dma=True, dsem=dsem)

    def emit(self, block, sems, final_waits):
        nc = self.nc
        for e in self.ENGS:
            c = 0
            for op in self.ops[e]:
                if op.signal:
                    c += 1
                    op.sig = c

        def run(engname, eng):
            waited = {}
            for op in self.ops[engname]:
                for d in op.deps:
                    key = d.eng
                    if waited.get(key, 0) < d.sig:
                        eng.wait_ge(sems[d.eng], d.sig)
                        waited[key] = d.sig
                for (ds, val) in op.dwait:
                    key = id(ds)
                    if waited.get(key, 0) < val:
                        eng.wait_ge(ds.sem, val)
                        waited[key] = val
                ins = op.fn(eng)
                if op.dma:
                    ins.then_inc(op.dsem.sem, 16)
                elif op.signal:
                    ins.then_inc(sems[engname], 1)
            if engname == "sp":
                for ds in final_waits:
                    eng.wait_ge(ds.sem, ds.count)

        @block.tensor
        def _(eng):
            run("pe", eng)

        @block.scalar
        def _(eng):
            run("act", eng)

        @block.vector
        def _(eng):
            run("dve", eng)

        @block.gpsimd
        def _(eng):
            run("pool", eng)

        @block.sync
        def _(eng):
            run("sp", eng)


def _param_layout():
    off = {}
    c = 0
    for l in range(DEPTH):
        for name, n in (("g1", 8), ("gm", 8), ("g2", 8), ("bmix", NBLK), ("hgn", 1), ("lcw", 32),
                        ("lcb", 8), ("lgb", 16), ("lam", 8), ("cvw", 8 * 31), ("cvb", 8), ("lng", 8),
                        ("lnb", 8)):
            off[(name, l)] = c
            c += n
    off["gf"] = c
    c += 8
    off["logits"] = c
    c += 32
    off["lam_all"] = c
    c += 32
    return off, c


POFF, NPAR = _param_layout()


def _fm(v):
    v = np.asarray(v, dtype=np.float32)
    return np.ascontiguousarray(v.reshape(-1, 128).T)


def pack_params(inp):
    P = np.zeros((128, NPAR), np.float32)

    def put(key, arr):
        o = POFF[key]
        P[:, o:o + arr.shape[1]] = arr

    for l in range(DEPTH):
        put(("g1", l), _fm(inp["norm_ffn1"][l]))
        put(("gm", l), _fm(inp["norm_mix"][l]))
        put(("g2", l), _fm(inp["norm_ffn2"][l]))
        put(("bmix", l), _fm(inp["b_in_mix"][l]))
        put(("hgn", l), np.asarray(inp["hg_norm"][l], np.float32).reshape(128, 1))
        w = np.asarray(inp["lru_conv_w"][l], np.float32).reshape(4, 8, 128)
        put(("lcw", l), np.ascontiguousarray(w.transpose(2, 1, 0)).reshape(128, 32))
        put(("lcb", l), _fm(inp["lru_conv_b"][l]))
        gb = np.asarray(inp["lru_gate_b"][l], np.float32).reshape(2, 8, 128)
        put(("lgb", l), np.ascontiguousarray(gb.transpose(2, 0, 1)).reshape(128, 16))
        put(("lam", l), _fm(inp["lru_lambda"][l]))
        cw = np.asarray(inp["cv_dw_w"][l], np.float32).reshape(31, 8, 128)
        put(("cvw", l), np.ascontiguousarray(cw.transpose(2, 1, 0)).reshape(128, 248))
        put(("cvb", l), _fm(inp["cv_dw_b"][l]))
        put(("lng", l), _fm(inp["cv_ln_g"][l]))
        put(("lnb", l), _fm(inp["cv_ln_b"][l]))
    put("gf", _fm(inp["norm_final"]))
    lg = np.asarray(inp["hgrn_lb_logits"], np.float32).reshape(4, 8, 128)
    put("logits", np.ascontiguousarray(lg.transpose(2, 1, 0)).reshape(128, 32))
    lam = np.asarray(inp["lru_lambda"], np.float32).reshape(4, 8, 128)
    put("lam_all", np.ascontiguousarray(lam.transpose(2, 0, 1)).reshape(128, 32))
    return P


def blockify(w, kc):
    w = np.asarray(w, np.float32)
    n = w.shape[1] // 128
    return np.ascontiguousarray(w.reshape(kc, 128, n, 128).transpose(2, 1, 0, 3))


def pack_weights(inp, nl):
    out = {}
    win1, wout1, win2, wout2, wmix, wbr, wom, wgate, bbc = [], [], [], [], [], [], [], [], []
    for l in range(nl):
        win1.append(blockify(inp["ffn1_w_in"][l], 8))
        win2.append(blockify(inp["ffn2_w_in"][l], 8))
        for src, dst in ((inp["ffn1_w_out"][l], wout1), (inp["ffn2_w_out"][l], wout2)):
            b = blockify(src, NFF)
            bp = np.zeros((8, 128, 24, 128), np.float32)
            bp[:, :, :NFF, :] = b
            dst.append(np.ascontiguousarray(bp.reshape(8, 128, 3, 8, 128).transpose(0, 2, 1, 3, 4)))
        wmix.append(blockify(inp["w_in_mix"][l], 8))
        wbr.append(np.stack([blockify(inp["w_branch"][l][b], 8) for b in range(3)]))
        wom.append(blockify(inp["w_out_mix"][l], 8))
        g = np.asarray(inp["lru_gate_w"][l], np.float32)
        wgate.append(np.ascontiguousarray(g.transpose(2, 0, 1, 3)).reshape(128, 16, 128))
        hib = np.asarray(inp["b_in_mix"][l][2048:3072], np.float32)
        bbc.append(np.ascontiguousarray(np.broadcast_to(hib[None, :], (128, 1024))))
    out["win1"] = np.stack(win1)
    out["win2"] = np.stack(win2)
    out["wout1"] = np.stack(wout1)
    out["wout2"] = np.stack(wout2)
    out["wmix"] = np.stack(wmix)
    out["wbr"] = np.stack(wbr)
    out["wom"] = np.stack(wom)
    out["wgate"] = np.stack(wgate)
    out["bbc"] = np.stack(bbc)
    return out


def make_consts():
    c = np.zeros((128, 256), np.float32)
    c[:, 0:128] = np.eye(128, dtype=np.float32)
    s = np.arange(128)[:, None]
    t = np.arange(128)[None, :]
    c[:, 128:256] = (s <= t).astype(np.float32)
    return c


def build(T, TT, NL, phases=("ffn1", "mix", "ffn2"), final_norm=True, nw=10, nf=20, nb=12, dbg=None, NP=1, weave=True, KPOOL=0, GPOOL=False):
    NT = T // TT
    NCH = TT // 128
    nc = bass.Bass("TRN2", target_bir_lowering=False)

    def dram(name, shape, kind="ExternalInput"):
        return nc.dram_tensor(name, list(shape), F32, kind=kind).ap()

    d_x = dram("xT", [128, 8, NP * T])
    d_par = dram("params", [128, NPAR])
    d_con = dram("consts", [128, 256])
    d_win = [dram("win1", [NL, 2 * NFF, 128, 8, 128]), dram("win2", [NL, 2 * NFF, 128, 8, 128])]
    d_wout = [dram("wout1", [NL, 8, 3, 128, 8, 128]), dram("wout2", [NL, 8, 3, 128, 8, 128])]
    d_wmix = dram("wmix", [NL, NBLK, 128, 8, 128])
    d_wbr = dram("wbr", [NL, 3, 8, 128, 8, 128])
    d_wom = dram("wom", [NL, 8, 128, 8, 128])
    d_wgate = dram("wgate", [NL, 128, 16, 128])
    d_bbc = dram("bbc", [NL, 128, 1024])
    d_y = dram("yT", [128, 8, NP * T], kind="ExternalOutput")
    SW = 1024 + 8 + 24 + 240
    d_st = nc.dram_tensor("st_scratch", [max(NL, 1), 128, SW], F32, kind="Internal").ap()

    P = Prog(nc)
    nsem = [0]
    wspec = {"win1": (d_win[0], [2 * NFF, 128, 8, 128]), "wout1": (d_wout[0], [8, 3, 128, 8, 128]),
             "wmix": (d_wmix, [NBLK, 128, 8, 128]), "wbr": (d_wbr, [3, 8, 128, 8, 128]),
             "wom": (d_wom, [8, 128, 8, 128]), "win2": (d_win[1], [2 * NFF, 128, 8, 128]),
             "wout2": (d_wout[1], [8, 3, 128, 8, 128])}
    d_cvd = nc.dram_tensor("bf_cvd", [max(NL, 1), 8, 4, 128, 8, 128], BF16, kind="Internal").ap()
    d_stc = nc.dram_tensor("st_cub", [max(NL, 1), 128, 240], BF16, kind="Internal").ap()
    wbf = {}
    for nm, (src, shp) in wspec.items():
        wbf[nm] = nc.dram_tensor("bf_" + nm, [max(NL, 1)] + shp, BF16, kind="Internal").ap()

    def new_sem(name):
        nsem[0] += 1
        return nc.alloc_semaphore(name)

    def sb(name, shape, dt=F32):
        return Tile(nc.alloc_sbuf_tensor("sb_" + name, list(shape), dt)[:], name)

    def ps(name, shape, dt=F32):
        return Tile(nc.alloc_psum_tensor("ps_" + name, list(shape), dt)[:], name)

    sems = {e: new_sem("s_" + e) for e in ("pe", "act", "dve", "pool")}
    ds_par = DSem(new_sem("d_par"))
    ds_x = DSem(new_sem("d_x"))
    ds_y = DSem(new_sem("d_y"))
    ds_bbc = DSem(new_sem("d_bbc"))
    ds_gate = DSem(new_sem("d_gate"))
    ds_sts = DSem(new_sem("d_sts"))
    ds_stl = DSem(new_sem("d_stl"))
    st_t = [Tile(d_st[l], f"st{l}") for l in range(max(NL, 1))]

    xt = [sb(f"x{j}", [128, 8, TT]) for j in range(NT)]
    par = sb("par", [128, NPAR])
    npar = sb("npar", [128, NPAR])
    con = sb("con", [128, 256])
    ident = sb("ident", [128, 128], BF16)
    mask = con
    ones = sb("ones", [128, 128])
    onesb = sb("onesb", [128, 128], BF16)
    smask = sb("smask", [128, TT])
    eps6 = sb("eps6", [128, 1])
    eps5 = sb("eps5", [128, 1])
    one1 = sb("one1", [128, 1])
    der = sb("der", [128, 4 * 8 * 4])
    uns = [sb("un0", [128, 8, TT], BF16), sb("un1", [128, 8, TT], BF16)]
    UN = [uns[0]]
    S_all = nc.alloc_sbuf_tensor("sb_S", [128, 8, 128], F32)[:]
    S = [Tile(S_all[:, h, :], f"S{h}") for h in range(8)]
    Sb = [sb(f"Sb{h}", [128, 128], BF16) for h in range(8)]
    hc_all = nc.alloc_sbuf_tensor("sb_hc", [128, 8], F32)[:]
    hc = [Tile(hc_all[:, h:h + 1], f"hc{h}") for h in range(8)]
    lxb_all = nc.alloc_sbuf_tensor("sb_lxb", [128, 8, 3 + TT], F32)[:]
    lxb = [Tile(lxb_all[:, h, :], f"lxb{h}") for h in range(8)]
    cub_all = nc.alloc_sbuf_tensor("sb_cub", [128, 8, 30 + TT], BF16)[:]
    cub = [Tile(cub_all[:, h, :], f"cub{h}") for h in range(8)]
    bbc = sb("bbc", [128, 1024])
    gw = sb("gw", [128, 16, 128], BF16)
    ybr = [sb(f"ybr{h}", [128, TT], BF16) for h in range(8)]
    mrg = [sb(f"mrg{m}", [128, TT]) for m in range(8)]
    mbf = [sb(f"mbf{m}", [128, TT], BF16) for m in range(8)]
    cv = [sb(f"cv{h}", [128, TT]) for h in range(8)]
    hT = [sb(f"hT{j}", [128, TT], BF16) for j in range(NFF)]
    ybrs = [ybr, hT[0:8], hT[8:16]]

    wslots = []
    for i in range(nw):
        t = sb(f"w{i}", [128, 8, 128], BF16)
        wslots.append((t, DSem(new_sem(f"d_w{i}"))))
    class Ctx:
        pass

    MAIN = Ctx()
    MAIN.WP = FreeList(wslots)
    MAIN.FP = FreeList([sb(f"f{i}", [128, TT]) for i in range(nf)])
    MAIN.BP = FreeList([sb(f"b{i}", [128, TT], BF16) for i in range(nb)])
    MAIN.PP = FreeList([ps(f"p{i}", [128, 512]) for i in range(7)])
    C = [MAIN]

    def sub_ctx(nfp, nbp, npp, nwp):
        c = Ctx()
        c.FP = FreeList([MAIN.FP.get() for _ in range(nfp)])
        c.BP = FreeList([MAIN.BP.get() for _ in range(nbp)])
        c.PP = FreeList([MAIN.PP.get() for _ in range(npp)])
        c.WP = FreeList([MAIN.WP.get() for _ in range(nwp)])
        return c

    def free_ctx(c):
        for name in ("FP", "BP", "PP", "WP"):
            fl = getattr(c, name)
            while fl.free:
                getattr(MAIN, name).put(fl.get())
    pT = ps("pT", [128, 1024], BF16)
    SCM = FreeList([sb(f"scm{i}", [128, TT], BF16) for i in range(2)])
    for t_ in list(SCM.free):
        P.memset("dve", t_.v, 0.0)

    def pcol(key, n=1, o=0):
        c = POFF[key] + o
        return par[:, c:c + n]

    def ncol(key, n=1, o=0):
        c = POFF[key] + o
        return npar[:, c:c + n]

    def sigmoid(out, in_, nbias=0.0, scale=1.0):
        P.act(out, in_, AF.Exp, bias=nbias, scale=-scale)
        P.act(out, out, AF.Ln, bias=one1.v)
        P.act(out, out, AF.Exp, scale=-1.0)

    wtile = {}
    wbf["cvd"] = d_cvd
    for l in range(NL):
        for nm, (src, shp) in wspec.items():
            if ("ffn1" not in phases and nm in ("win1", "wout1")) or ("ffn2" not in phases and nm in ("win2", "wout2")) \
                    or ("mix" not in phases and nm in ("wmix", "wbr", "wom")):
                continue
            dsw = DSem(new_sem(f"d_cv_{nm}{l}"))
            tl = Tile(wbf[nm][l], f"bf_{nm}{l}")
            wtile[(nm, l)] = tl
            nblk = 1
            for d_ in shp[:-3]:
                nblk *= d_
            srcf = src[l].rearrange("a b p k c -> (a b) p k c") if len(shp) == 5 else src[l]
            dstf = wbf[nm][l].rearrange("a b p k c -> (a b) p k c") if len(shp) == 5 else wbf[nm][l]
            for b0 in range(0, nblk, 8):
                b1 = min(nblk, b0 + 8)
                P.rec("pool", lambda e, o=dstf[b0:b1], i=srcf[b0:b1]: e.dma_start(out=o, in_=i), [], [tl], dma=True, dsem=dsw)
    P.dma("sp", par.v, d_par, ds_par)
    P.dma("sp", con.v, d_con, ds_par)
    P.ts("dve", npar.v, par.v, -1.0, None, ALU.mult)
    P.copy("dve", ident.v, con[:, 0:128])
    P.memset("dve", ones.v, 1.0)
    P.memset("dve", onesb.v, 1.0)
    P.memset("dve", eps6.v, 1e-6)
    P.memset("dve", eps5.v, 1e-5)
    P.memset("dve", one1.v, 1.0)
    P.memset("dve", smask.v, 1.0)
    for c in range(NCH):
        P.memset("dve", smask[:, c * 128:c * 128 + 1], 0.0)

    def state_zero():
        P.rec("dve", lambda e: e.memset(S_all, 0.0), [], S)
        P.rec("dve", lambda e: e.memset(hc_all, 0.0), [], hc)
        P.rec("dve", lambda e: e.memset(lxb_all[:, :, 0:3], 0.0), [], lxb)
        P.rec("dve", lambda e: e.memset(cub_all[:, :, 0:30], 0.0), [], cub)
        for h in range(8):
            P.memset("dve", Sb[h].v, 0.0)

    def state_save(l):
        d = d_st[l]
        P.rec("sp", lambda e: e.dma_start(out=d[:, 0:1024].rearrange("p (h v) -> p h v", v=128), in_=S_all),
              S, [st_t[l]], dma=True, dsem=ds_sts)
        P.rec("sp", lambda e: e.dma_start(out=d[:, 1024:1032], in_=hc_all), hc, [st_t[l]], dma=True, dsem=ds_sts)
        P.rec("sp", lambda e: e.dma_start(out=d[:, 1032:1056].rearrange("p (h j) -> p h j", j=3), in_=lxb_all[:, :, 0:3]),
              lxb, [st_t[l]], dma=True, dsem=ds_sts)
        P.rec("sp", lambda e: e.dma_start(out=d_stc[l].rearrange("p (h j) -> p h j", j=30), in_=cub_all[:, :, 0:30]),
              cub, [st_t[l]], dma=True, dsem=ds_sts)

    def state_load(l):
        d = d_st[l]
        P.rec("sp", lambda e: e.dma_start(out=S_all, in_=d[:, 0:1024].rearrange("p (h v) -> p h v", v=128)),
              [st_t[l]], S, dma=True, dsem=ds_stl)
        P.rec("sp", lambda e: e.dma_start(out=hc_all, in_=d[:, 1024:1032]), [st_t[l]], hc, dma=True, dsem=ds_stl)
        P.rec("sp", lambda e: e.dma_start(out=lxb_all[:, :, 0:3], in_=d[:, 1032:1056].rearrange("p (h j) -> p h j", j=3)),
              [st_t[l]], lxb, dma=True, dsem=ds_stl)
        P.rec("sp", lambda e: e.dma_start(out=cub_all[:, :, 0:30], in_=d_stc[l].rearrange("p (h j) -> p h j", j=30)),
              [st_t[l]], cub, dma=True, dsem=ds_stl)
        for h in range(8):
            P.copy("act", Sb[h].v, S[h].v)

    def dcol(kind, l, c=0, n=8):
        o = kind * 32 + l * 8 + c
        return der[:, o:o + n]

    ft = C[0].FP.get()
    ex = ft[:, 0:32]
    P.act(ex, pcol("logits", 32), AF.Exp)
    sm = ft[:, 32:40]
    ex3 = TV(ft, ft.ap[:, 0:32].rearrange("p (c l) -> p c l", l=4))
    P.reduce_sum("dve", sm, ex3)
    rc = ft[:, 40:48]
    P.recip(rc, sm)
    P.memset("dve", dcol(0, 0), 0.0)
    for l in range(1, 4):
        exl = TV(ft, ft.ap[:, 0:32].rearrange("p (c l) -> p l c", l=4)[:, l, :])
        P.tt("dve", ft[:, 48:56], exl, rc, ALU.mult)
        P.tt("dve", dcol(0, l), dcol(0, l - 1), ft[:, 48:56], ALU.add)
    P.ts("dve", der[:, 32:64], der[:, 0:32], -1.0, 1.0, ALU.mult, ALU.add)
    P.ts("dve", der[:, 64:96], der[:, 0:32], 1.0, -1.0, ALU.mult, ALU.add)
    P.act(ft[:, 64:96], pcol("lam_all", 32), AF.Exp, scale=-1.0)
    P.act(ft[:, 64:96], ft[:, 64:96], AF.Ln, bias=one1.v)
    P.ts("dve", der[:, 96:128], ft[:, 64:96], -8.0, None, ALU.mult)
    C[0].FP.put(ft)

    if "mix" in phases:
        for l in range(NL):
            dsd = DSem(new_sem(f"d_cvd{l}"))
            tl = Tile(d_cvd[l], f"cvd{l}")
            wtile[("cvd", l)] = tl
            for h in range(8):
                for g in range(4):
                    stg, dsg = MAIN.WP.get()
                    for jj in range(8):
                        jx = g * 8 + jj
                        if jx < 31:
                            wcol = POFF[("cvw", l)] + h * 31 + jx
                            P.act(stg[:, jj, :], ident.v, AF.Copy, scale=par[:, wcol:wcol + 1])
                        else:
                            P.memset("dve", stg[:, jj, :], 0.0)
                    P.rec("sp", lambda e, o=d_cvd[l, h, g], i=stg.ap: e.dma_start(out=o, in_=i), [stg], [tl], dma=True, dsem=dsd)
                    MAIN.WP.put((stg, dsg))

    def load_w(src):
        nm, l, idx = src
        ap = wbf[nm][l]
        for i_ in idx:
            ap = ap[i_]
        t, ds = C[0].WP.get()
        P.rec("sp", lambda e, o=t.ap, i=ap: e.dma_start(out=o, in_=i), [wtile[(nm, l)]], [t], dma=True, dsem=ds)
        return t, ds

    def rmsnorm_to_un(j, gkey, un):
        pss = C[0].PP.get()
        for c in range(8):
            sq = C[0].FP.get()
            sqb = TV(sq, sq.ap.bitcast(BF16)[:, 0:TT])
            P.act(sqb, xt[j][:, c, :], AF.Square)
            P.mm(pss[:, 0:TT], onesb.v, sqb, c == 0, c == 7)
            C[0].FP.put(sq)
        rs = C[0].FP.get()
        P.act(rs.v, pss[:, 0:TT], AF.Ln, bias=eps6.v, scale=1.0 / D)
        C[0].PP.put(pss)
        P.act(rs.v, rs.v, AF.Exp, scale=-0.5)
        for c in range(8):
            P.stt("dve", un[:, c, :], xt[j][:, c, :], pcol(gkey, 1, c), rs.v, ALU.mult, ALU.mult)
        C[0].FP.put(rs)

    def proj(src):
        w, ds = load_w(src)
        p = C[0].PP.get()
        for c in range(8):
            P.mm(p[:, 0:TT], w[:, c, :], UN[0][:, c, :], c == 0, c == 7)
        C[0].WP.put((w, ds))
        return p

    def ffn(l, which, j):
        dwin = "win1" if which == 0 else "win2"
        dwout = "wout1" if which == 0 else "wout2"
        for jj in range(NFF):
            pg = proj((dwin, l, (jj,)))
            pu = proj((dwin, l, (NFF + jj,)))
            sg = C[0].FP.get()
            sigmoid(sg.v, pg[:, 0:TT])
            P.tt("dve", sg.v, sg.v, pg[:, 0:TT], ALU.mult)
            C[0].PP.put(pg)
            P.tt("dve", hT[jj].v, sg.v, pu[:, 0:TT], ALU.mult)
            C[0].FP.put(sg)
            C[0].PP.put(pu)
        for m in range(8):
            py = C[0].PP.get()
            for piece in range(3):
                w, ds = load_w((dwout, l, (m, piece)))
                nk = 8 if piece < 2 else NFF - 16
                for k in range(nk):
                    kk = piece * 8 + k
                    P.mm(py[:, 0:TT], w[:, k, :], hT[kk].v, kk == 0, kk == NFF - 1)
                C[0].WP.put((w, ds))
            P.stt("dve", xt[j][:, m, :], py[:, 0:TT], 0.5, xt[j][:, m, :], ALU.mult, ALU.add)
            C[0].PP.put(py)

    def branch_project(l, b, first, last):
        for m in range(8):
            w, ds = load_w(("wbr", l, (b, m)))
            pb = C[0].PP.get()
            for k in range(8):
                P.mm(pb[:, 0:TT], w[:, k, :], ybrs[b][k].v, k == 0, k == 7)
            C[0].WP.put((w, ds))
            blk = 64 + 8 * b + m
            pgt = proj(("wmix", l, (blk,)))
            gt = C[0].FP.get()
            sigmoid(gt.v, pgt[:, 0:TT], ncol(("bmix", l), 1, blk))
            C[0].PP.put(pgt)
            if first:
                P.tt("dve", mrg[m].v, pb[:, 0:TT], gt.v, ALU.mult)
            else:
                P.tt("dve", gt.v, pb[:, 0:TT], gt.v, ALU.mult)
                if last:
                    P.tt("dve", mbf[m].v, mrg[m].v, gt.v, ALU.add)
                else:
                    P.tt("dve", mrg[m].v, mrg[m].v, gt.v, ALU.add)
            C[0].PP.put(pb)
            C[0].FP.put(gt)

    def hgrn_head(l, h, j):
        bm = ("bmix", l)
        lb = dcol(0, l, h, 1)
        oml = dcol(1, l, h, 1)
        noml = dcol(2, l, h, 1)
        pz = proj(("wmix", l, (8 + h,)))
        sig = C[0].FP.get()
        sigmoid(sig.v, pz[:, 0:TT], ncol(bm, 1, 8 + h))
        C[0].PP.put(pz)
        lf = C[0].FP.get()
        P.ts("dve", lf.v, sig.v, oml, lb, ALU.mult, ALU.add)
        P.act(lf.v, lf.v, AF.Ln)
        kk = C[0].FP.get()
        P.ts("dve", kk.v, sig.v, noml, oml, ALU.mult, ALU.add)
        C[0].FP.put(sig)
        bcum = C[0].FP.get()
        P.scan("dve", bcum.v, smask.v, lf.v, 0.0, ALU.mult, ALU.add)
        C[0].FP.put(lf)
        b3 = TV(bcum, bcum.ap.rearrange("p (c t) -> p c t", t=128))
        e1 = C[0].FP.get()
        e13 = TV(e1, e1.ap.rearrange("p (c t) -> p c t", t=128))
        P.tt("dve", e13, b3, b3[:, :, 63:64].bc([128, NCH, 128]), ALU.subtract)
        e2 = C[0].FP.get()
        P.act(e2.v, e1.v, AF.Exp, scale=-1.0)
        P.act(e1.v, e1.v, AF.Exp)
        e4 = C[0].FP.get()
        e43 = TV(e4, e4.ap.rearrange("p (c t) -> p c t", t=128))
        P.tt("dve", e43, b3[:, :, 127:128].bc([128, NCH, 128]), b3, ALU.subtract)
        P.act(e4.v, e4.v, AF.Exp)
        P.act(bcum.v, bcum.v, AF.Exp)
        e3 = bcum
        pq = proj(("wmix", l, (h,)))
        q = C[0].FP.get()
        sigmoid(q.v, pq[:, 0:TT], ncol(bm, 1, h))
        P.stt("dve", q.v, pq[:, 0:TT], pcol(bm, 1, h), q.v, ALU.add, ALU.mult)
        C[0].PP.put(pq)
        qt = C[0].BP.get(); kt = C[0].BP.get(); qh = C[0].BP.get(); kh = C[0].BP.get()
        P.tt("dve", qt.v, q.v, e1.v, ALU.mult)
        P.tt("dve", kt.v, kk.v, e2.v, ALU.mult)
        P.tt("dve", qh.v, q.v, e3.v, ALU.mult)
        P.tt("dve", kh.v, kk.v, e4.v, ALU.mult)
        C[0].FP.put(q); C[0].FP.put(kk); C[0].FP.put(e1); C[0].FP.put(e2); C[0].FP.put(e4)
        w, ds = load_w(("wmix", l, (16 + h,)))
        pv = C[0].PP.get()
        for c in range(NCH):
            for k in range(8):
                P.mm(pv[:, c * 128:(c + 1) * 128], UN[0][:, k, c * 128:(c + 1) * 128], w[:, k, :], k == 0, k == 7)
        C[0].WP.put((w, ds))
        V = C[0].BP.get()
        pv3 = TV(pv, pv.ap[:, 0:TT].rearrange("p (c v) -> p c v", v=128))
        V3 = TV(V, V.ap.rearrange("p (c v) -> p c v", v=128))
        P.tt("dve", V3, pv3,
             TV(bbc, bbc.ap[:, h * 128:(h + 1) * 128].rearrange("p (o v) -> p o v", o=1).to_broadcast([128, NCH, 128])),
             ALU.add)
        C[0].PP.put(pv)
        for c in range(NCH):
            P.transpose(pT[:, c * 128:(c + 1) * 128], kh[:, c * 128:(c + 1) * 128], ident.v)
        khT = C[0].BP.get()
        P.copy("dve", khT.v, pT[:, 0:TT])
        C[0].BP.put(kh)
        psc = C[0].PP.get()
        for c in range(NCH):
            P.mm(psc[:, c * 128:(c + 1) * 128], kt[:, c * 128:(c + 1) * 128], qt[:, c * 128:(c + 1) * 128], True, True)
        scm = SCM.get()
        mk = TV(con, con.ap[:, 128:256].bitcast(mybir.dt.uint32).rearrange("p (o t) -> p o t", o=1).to_broadcast([128, NCH, 128]))
        so = TV(scm, scm.ap.rearrange("p (c t) -> p c t", t=128))
        si = TV(psc, psc.ap[:, 0:TT].rearrange("p (c t) -> p c t", t=128))
        P.rec("dve", lambda e, o=so.ap, m=mk.ap, d=si.ap: e.copy_predicated(o, m, d), [con, psc], [scm])
        C[0].PP.put(psc)
        C[0].BP.put(qt); C[0].BP.put(kt)
        po = C[0].PP.get()
        for c in range(NCH):
            cs = slice(c * 128, (c + 1) * 128)
            P.mm(po[:, cs], V[:, cs], scm[:, cs], True, False)
            P.mm(po[:, cs], Sb[h].v, qh[:, cs], False, True)
            pS = C[0].PP.get()
            P.mm(pS[:, 0:128], khT[:, cs], V[:, cs], True, True)
            P.stt("dve", S[h].v, S[h].v, e3[:, c * 128 + 127:c * 128 + 128], pS[:, 0:128], ALU.mult, ALU.add)
            C[0].PP.put(pS)
            P.copy("act", Sb[h].v, S[h].v)
        C[0].BP.put(V); SCM.put(scm); C[0].BP.put(qh); C[0].BP.put(khT)
        C[0].FP.put(e3)
        if dbg == "hg0":
            P.copy("act", ybrs[0][h].v, po[:, 0:TT])
            C[0].PP.put(po)
            return
        osq = C[0].FP.get()
        osqb = TV(osq, osq.ap.bitcast(BF16)[:, 0:TT])
        P.act(osqb, po[:, 0:TT], AF.Square)
        pss = C[0].PP.get()
        P.mm(pss[:, 0:TT], onesb.v, osqb, True, True)
        rs = osq
        P.act(rs.v, pss[:, 0:TT], AF.Ln, bias=eps6.v, scale=1.0 / 128)
        C[0].PP.put(pss)
        P.act(rs.v, rs.v, AF.Exp, scale=-0.5)
        pg = proj(("wmix", l, (24 + h,)))
        sg = C[0].FP.get()
        sigmoid(sg.v, pg[:, 0:TT], ncol(bm, 1, 24 + h))
        P.stt("dve", sg.v, pg[:, 0:TT], pcol(bm, 1, 24 + h), sg.v, ALU.add, ALU.mult)
        C[0].PP.put(pg)
        P.stt("dve", rs.v, po[:, 0:TT], pcol(("hgn", l)), rs.v, ALU.mult, ALU.mult)
        C[0].PP.put(po)
        P.tt("dve", ybrs[0][h].v, rs.v, sg.v, ALU.mult)
        C[0].FP.put(rs); C[0].FP.put(sg)

    def lru_block(l, h, j):
        bm = ("bmix", l)
        plx = proj(("wmix", l, (32 + h,)))
        P.ts("dve", lxb[h][:, 3:3 + TT], plx[:, 0:TT], pcol(bm, 1, 32 + h), None, ALU.add)
        C[0].PP.put(plx)
        xb = C[0].FP.get()
        wc = POFF[("lcw", l)] + h * 4
        P.ts("dve", xb.v, lxb[h][:, 3:3 + TT], par[:, wc + 3:wc + 4], pcol(("lcb", l), 1, h), ALU.mult, ALU.add)
        for jx in (2, 1, 0):
            P.stt("dve", xb.v, lxb[h][:, jx:jx + TT], par[:, wc + jx:wc + jx + 1], xb.v, ALU.mult, ALU.add)
        P.copy("dve", lxb[h][:, 0:3], lxb[h][:, TT:TT + 3])
        xbb = C[0].BP.get()
        P.copy("dve", xbb.v, xb.v)
        pr = C[0].PP.get()
        P.mm(pr[:, 0:TT], gw[:, h, :], xbb.v, True, True)
        pi = C[0].PP.get()
        P.mm(pi[:, 0:TT], gw[:, 8 + h, :], xbb.v, True, True)
        C[0].BP.put(xbb)
        a = C[0].FP.get()
        sigmoid(a.v, pr[:, 0:TT], ncol(("lgb", l), 1, h))
        C[0].PP.put(pr)
        it = C[0].FP.get()
        sigmoid(it.v, pi[:, 0:TT], ncol(("lgb", l), 1, 8 + h))
        C[0].PP.put(pi)
        P.act(a.v, a.v, AF.Exp, scale=dcol(3, l, h, 1))
        mt = C[0].FP.get()
        P.tt("dve", mt.v, a.v, a.v, ALU.mult)
        P.act(mt.v, mt.v, AF.Ln, bias=one1.v, scale=-1.0)
        P.act(mt.v, mt.v, AF.Exp, scale=0.5)
        P.tt("dve", it.v, it.v, xb.v, ALU.mult)
        P.tt("dve", it.v, it.v, mt.v, ALU.mult)
        C[0].FP.put(xb)
        hs = mt
        P.scan("dve", hs.v, a.v, it.v, hc[h].v, ALU.mult, ALU.add)
        P.copy("dve", hc[h].v, hs[:, TT - 1:TT])
        C[0].FP.put(a); C[0].FP.put(it)
        plg = proj(("wmix", l, (40 + h,)))
        lg = C[0].FP.get()
        P.ts("dve", lg.v, plg[:, 0:TT], pcol(bm, 1, 40 + h), None, ALU.add)
        C[0].PP.put(plg)
        t = C[0].FP.get()
        ge = "pool" if GPOOL else "dve"
        P.tt(ge, t.v, lg.v, lg.v, ALU.mult)
        P.ts(ge, t.v, t.v, 0.044715, 1.0, ALU.mult, ALU.add)
        P.tt(ge, t.v, t.v, lg.v, ALU.mult)
        sigmoid(t.v, t.v, 0.0, 1.5957691216057308)
        P.tt(ge, t.v, t.v, lg.v, ALU.mult)
        P.tt("dve", ybrs[1][h].v, t.v, hs.v, ALU.mult)
        C[0].FP.put(lg); C[0].FP.put(t); C[0].FP.put(hs)

    def conv_block(l, h, j):
        bm = ("bmix", l)
        pca = proj(("wmix", l, (48 + h,)))
        pcb = proj(("wmix", l, (56 + h,)))
        sg = C[0].FP.get()
        sigmoid(sg.v, pcb[:, 0:TT], ncol(bm, 1, 56 + h))
        C[0].PP.put(pcb)
        P.stt("dve", cub[h][:, 30:30 + TT], pca[:, 0:TT], pcol(bm, 1, 48 + h), sg.v, ALU.add, ALU.mult)
        C[0].PP.put(pca)
        C[0].FP.put(sg)
        pcv = C[0].PP.get()
        for g in range(4):
            w, ds = load_w(("cvd", l, (h, g)))
            for jj in range(8):
                jx = g * 8 + jj
                if jx < 31:
                    P.mm(pcv[:, 0:TT], w[:, jj, :], cub[h][:, jx:jx + TT], jx == 0, jx == 30)
            C[0].WP.put((w, ds))
        P.act(cv[h].v, pcv[:, 0:TT], AF.Identity, bias=pcol(("cvb", l), 1, h))
        C[0].PP.put(pcv)
        P.copy("dve", cub[h][:, 0:30], cub[h][:, TT:TT + 30])

    def conv_finish(l, j):
        import os
        stage = int(os.environ.get("DBGSTAGE", "9"))
        pm = C[0].PP.get()
        pq = C[0].PP.get()
        for h in range(8):
            P.mm(pm[:, 0:TT], ones.v, cv[h].v, h == 0, h == 7)
        for h in range(8):
            sq = C[0].FP.get()
            sqb = TV(sq, sq.ap.bitcast(BF16)[:, 0:TT])
            P.act(sqb, cv[h].v, AF.Square)
            P.mm(pq[:, 0:TT], onesb.v, sqb, h == 0, h == 7)
            C[0].FP.put(sq)
        mu = C[0].FP.get()
        P.ts("dve", mu.v, pm[:, 0:TT], 1.0 / D, None, ALU.mult)
        C[0].PP.put(pm)
        rs = C[0].FP.get()
        P.tt("dve", rs.v, mu.v, mu.v, ALU.mult)
        if stage >= 2:
            P.stt("dve", rs.v, pq[:, 0:TT], 1.0 / D, rs.v, ALU.mult, ALU.subtract)
        C[0].PP.put(pq)
        if stage >= 3:
            P.act(rs.v, rs.v, AF.Ln, bias=eps5.v)
            P.act(rs.v, rs.v, AF.Exp, scale=-0.5)
        for h in range(8):
            t = C[0].FP.get()
            P.tt("dve", t.v, cv[h].v, mu.v, ALU.subtract)
            P.stt("dve", t.v, t.v, pcol(("lng", l), 1, h), rs.v, ALU.mult, ALU.mult)
            sg = C[0].FP.get()
            sigmoid(sg.v, t.v, ncol(("lnb", l), 1, h))
            P.stt("dve", ybrs[2][h].v, t.v, pcol(("lnb", l), 1, h), sg.v, ALU.add, ALU.mult)
            C[0].FP.put(sg)
            C[0].FP.put(t)
        C[0].FP.put(mu); C[0].FP.put(rs)

    def mixer(l, j, p=0, nxt=None):
        if dbg is not None:
            for h in range(8):
                {"hg": hgrn_head, "hg0": hgrn_head, "lru": lru_block, "cv": conv_block, "cv0": conv_block}[dbg](l, h, j)
            if dbg == "cv":
                conv_finish(l, j)
            for h in range(8):
                t = C[0].FP.get()
                P.copy("dve", t.v, cv[h].v if dbg == "cv0" else ybrs[{"hg": 0, "hg0": 0, "lru": 1, "cv": 2}[dbg]][h].v)
                P.dma("sp", d_y[:, h, p * T + j * TT:p * T + (j + 1) * TT], t.v, ds_y)
                C[0].FP.put(t)
            return
        if weave:
            ctxs = [sub_ctx(9, 7, 3, 4), sub_ctx(7, 2, 2, 3), sub_ctx(4, 0, 2, 3)]
            for h in range(8):
                sts = []
                for fn_, cx in ((hgrn_head, ctxs[0]), (lru_block, ctxs[1]), (conv_block, ctxs[2])):
                    C[0] = cx
                    P.begin_stream()
                    fn_(l, h, j)
                    sts.append(P.end_stream())
                C[0] = MAIN
                P.merge(sts)
            for cx in ctxs:
                free_ctx(cx)
        else:
            for h in range(8):
                hgrn_head(l, h, j)
            for h in range(8):
                lru_block(l, h, j)
            for h in range(8):
                conv_block(l, h, j)
        ctxr = sub_ctx(3, 0, 1, 0) if nxt is not None else None
        if nxt is not None:
            P.begin_stream()
        branch_project(l, 0, True, False)
        branch_project(l, 1, False, False)
        conv_finish(l, j)
        branch_project(l, 2, False, True)
        for m in range(8):
            w, ds = load_w(("wom", l, (m,)))
            pw = C[0].PP.get()
            for k in range(8):
                P.mm(pw[:, 0:TT], w[:, k, :], mbf[k].v, k == 0, k == 7)
            C[0].WP.put((w, ds))
            P.tt("dve", xt[j][:, m, :], pw[:, 0:TT], xt[j][:, m, :], ALU.add)
            C[0].PP.put(pw)
        if nxt is not None:
            sa = P.end_stream()
            C[0] = ctxr
            P.begin_stream()
            rmsnorm_to_un(nxt[0], nxt[1], nxt[2])
            sb_ = P.end_stream()
            C[0] = MAIN
            P.merge([sa, sb_])
            free_ctx(ctxr)

    def ffn_woven(l, which, j, nxt):
        if nxt is None:
            ffn(l, which, j)
            return
        ctxr = sub_ctx(3, 0, 1, 0)
        P.begin_stream()
        ffn(l, which, j)
        sa = P.end_stream()
        C[0] = ctxr
        P.begin_stream()
        rmsnorm_to_un(nxt[0], nxt[1], nxt[2])
        sb_ = P.end_stream()
        C[0] = MAIN
        P.merge([sa, sb_])
        free_ctx(ctxr)

    for p in range(NP):
        for j in range(NT):
            P.dma("sp", xt[j].v, d_x[:, :, p * T + j * TT:p * T + (j + 1) * TT], ds_x)
        seq = []
        for l in range(NL):
            for kind in ("ffn1", "mix", "ffn2"):
                if kind in phases:
                    for j in range(NT):
                        seq.append((kind, l, j))
        gk = {"ffn1": "g1", "mix": "gm", "ffn2": "g2"}
        for i, (kind, l, j) in enumerate(seq):
            if i == 0:
                rmsnorm_to_un(j, (gk[kind], l), uns[0])
            UN[0] = uns[i % 2]
            nxt = None
            if i + 1 < len(seq) and NT >= 2 and dbg is None:
                k2, l2, j2 = seq[i + 1]
                nxt = (j2, (gk[k2], l2), uns[(i + 1) % 2])
            elif i + 1 < len(seq):
                k2, l2, j2 = seq[i + 1]
                nxt = None
            if kind == "mix" and j == 0:
                P.dma("sp", bbc.v, d_bbc[l], ds_bbc)
                P.dma("pool", gw.v, d_wgate[l], ds_gate)
                if p == 0:
                    state_zero()
                else:
                    state_load(l)
            if kind == "mix":
                mixer(l, j, p, nxt)
                if j == NT - 1 and p < NP - 1:
                    state_save(l)
            else:
                ffn_woven(l, 0 if kind == "ffn1" else 1, j, nxt)
            if nxt is None and i + 1 < len(seq):
                k2, l2, j2 = seq[i + 1]
                rmsnorm_to_un(j2, (gk[k2], l2), uns[(i + 1) % 2])
        for j in range(NT):
            if final_norm:
                pss = C[0].PP.get()
                for c in range(8):
                    sq = C[0].FP.get()
                    P.act(sq.v, xt[j][:, c, :], AF.Square)
                    P.mm(pss[:, 0:TT], ones.v, sq.v, c == 0, c == 7)
                    C[0].FP.put(sq)
                rs = C[0].FP.get()
                P.act(rs.v, pss[:, 0:TT], AF.Ln, bias=eps6.v, scale=1.0 / D)
                C[0].PP.put(pss)
                P.act(rs.v, rs.v, AF.Exp, scale=-0.5)
                for c in range(8):
                    P.stt("dve", xt[j][:, c, :], xt[j][:, c, :], pcol("gf", 1, c), rs.v, ALU.mult, ALU.mult)
                C[0].FP.put(rs)
            if dbg is None:
                P.dma("sp", d_y[:, :, p * T + j * TT:p * T + (j + 1) * TT], xt[j].v, ds_y)

    with nc.Block() as block:
        P.emit(block, sems, [ds_y])
    return nc, P


N_CORES = 4
T_PASS = 2048
N_PASS = 2
TT_CORE = 256


def make_in_maps(inp, xs, nl):
    wts = pack_weights(inp, nl)
    par = pack_params(inp)
    con = make_consts()
    maps = []
    for xc in xs:
        T = xc.shape[0]
        xT = np.ascontiguousarray(np.asarray(xc, np.float32).reshape(T, 8, 128).transpose(2, 1, 0))
        m = {"xT": xT, "params": par, "consts": con}
        m.update(wts)
        maps.append(m)
    return maps


def kernel(**inp):
    x = np.asarray(inp["x"], np.float32)
    B, S, _ = x.shape
    xs = [x[b] for b in range(B)]
    nc, _ = build(T_PASS, TT_CORE, DEPTH, NP=N_PASS)
    in_maps = make_in_maps(inp, xs, DEPTH)
    res = run_bass_kernel_spmd(nc, in_maps, core_ids=list(range(len(xs))))
    outs = []
    for r in res.results:
        yT = np.asarray(r["yT"], np.float32)
        outs.append(yT.transpose(2, 1, 0).reshape(S, D))
    return np.stack(outs).astype(np.float32)
```
